# Optimizing a Trainium2 kernel written in Bass

```python
import jax
import jax.numpy as jnp
from jax import lax
import numpy as np

D_MODEL = 2048
BATCH = 1
SEQ = 8192
DEPTH = 4
DEC_BATCH = 8
DEC_SEQ = 4096
PAST_LEN = 128

HEAD_DIM = 128
A_Q_HEADS = 8
A_KV_HEADS = 2
B_Q_HEADS = 8
B_KV_HEADS = 2
A_Q_W = A_Q_HEADS * HEAD_DIM
A_KV_W = A_KV_HEADS * HEAD_DIM
B_Q_W = B_Q_HEADS * HEAD_DIM
B_KV_W = B_KV_HEADS * HEAD_DIM
IN_W = A_Q_W + 2 * A_KV_W + B_Q_W + 2 * B_KV_W + 2 * D_MODEL
WINDOW = 128
BLOCK = 128
GRID_W = 64
ROPE_THETA = 10000.0
D_FF = ((8 * D_MODEL + 3 * 256 - 1) // (3 * 256)) * 256
N_MOD = 6
EPS = 1e-6
MASK_VALUE = -1e30

kernel_name = "hybrid_gated_window_axial_encoder"


def rms_norm(x, g):
    xf = x.astype(jnp.float32)
    y = xf * lax.rsqrt(jnp.mean(xf * xf, axis=-1, keepdims=True) + EPS)
    return (y * g.astype(jnp.float32)).astype(x.dtype)


def rope_angles(pos, dim):
    inv_freq = ROPE_THETA ** (-jnp.arange(0, dim, 2, dtype=jnp.float32) / dim)
    ang = pos.astype(jnp.float32)[:, None] * inv_freq[None, :]
    return jnp.cos(ang), jnp.sin(ang)


def apply_rotary(x, cos, sin):
    c = cos[None, :, None, :].astype(x.dtype)
    s = sin[None, :, None, :].astype(x.dtype)
    x1, x2 = jnp.split(x, 2, axis=-1)
    return jnp.concatenate([x1 * c - x2 * s, x2 * c + x1 * s], axis=-1)


def apply_axial_rotary(x, cos_r, sin_r, cos_c, sin_c):
    xr, xc = jnp.split(x, 2, axis=-1)
    return jnp.concatenate([apply_rotary(xr, cos_r, sin_r), apply_rotary(xc, cos_c, sin_c)], axis=-1)


def window_attention(q, k, v, sink):
    B, S, Hq, Dh = q.shape
    Hkv = k.shape[2]
    G = Hq // Hkv
    nb = S // BLOCK
    pad = ((0, 0), (BLOCK, BLOCK), (0, 0), (0, 0))
    kp = jnp.pad(k, pad).reshape(B, nb + 2, BLOCK, Hkv, Dh)
    vp = jnp.pad(v, pad).reshape(B, nb + 2, BLOCK, Hkv, Dh)
    kb = jnp.concatenate([kp[:, :-2], kp[:, 1:-1], kp[:, 2:]], axis=2)
    vb = jnp.concatenate([vp[:, :-2], vp[:, 1:-1], vp[:, 2:]], axis=2)
    qb = q.reshape(B, nb, BLOCK, Hkv, G, Dh)
    s = jnp.einsum('bnqhgd,bnkhd->bnhgqk', qb, kb).astype(jnp.float32) * (Dh ** -0.5)
    qpos = jnp.arange(nb)[:, None] * BLOCK + jnp.arange(BLOCK)[None, :]
    kpos = jnp.arange(nb)[:, None] * BLOCK - BLOCK + jnp.arange(3 * BLOCK)[None, :]
    valid = ((jnp.abs(qpos[:, :, None] - kpos[:, None, :]) <= WINDOW)
             & (kpos[:, None, :] >= 0) & (kpos[:, None, :] < S))
    s = jnp.where(valid[None, :, None, None], s, MASK_VALUE)
    sink_l = sink.astype(jnp.float32).reshape(Hkv, G)[None, None, :, :, None, None]
    m = jnp.maximum(jnp.max(s, axis=-1, keepdims=True), sink_l)
    p = jnp.exp(s - m)
    denom = jnp.sum(p, axis=-1, keepdims=True) + jnp.exp(sink_l - m)
    p = (p / denom).astype(v.dtype)
    o = jnp.einsum('bnhgqk,bnkhd->bnqhgd', p, vb)
    return o.reshape(B, S, Hq * Dh)


def dense_attention(q, k, v):
    B, S, Hq, Dh = q.shape
    Hkv = k.shape[2]
    G = Hq // Hkv
    nb = S // BLOCK
    qb = q.reshape(B, nb, BLOCK, Hkv, G, Dh).transpose(1, 0, 2, 3, 4, 5)

    def one_block(qi):
        s = jnp.einsum('bqhgd,bkhd->bhgqk', qi, k).astype(jnp.float32) * (Dh ** -0.5)
        p = jax.nn.softmax(s, axis=-1).astype(v.dtype)
        return jnp.einsum('bhgqk,bkhd->bqhgd', p, v)

    o = lax.map(one_block, qb)
    return o.transpose(1, 0, 2, 3, 4, 5).reshape(B, S, Hq * Dh)


def encoder_layer(x, c, rope1d, rope_axial, g_pre_mix, g_post_mix, g_pre_ffn, g_post_ffn, w_mod, b_mod,
                  w_in, q_norm_b, k_norm_b, sink_a, w_branch_a, w_branch_b, w_out, w_13, w_2):
    B, S, _ = x.shape
    cos1, sin1 = rope1d
    cos_r, sin_r, cos_c, sin_c = rope_axial
    mod = jax.nn.silu(c) @ w_mod + b_mod
    shift_m, scale_m, gate_m, shift_f, scale_f, gate_f = [m[:, None, :] for m in jnp.split(mod, N_MOD, axis=-1)]

    u = rms_norm(x, g_pre_mix) * (1 + scale_m) + shift_m
    z = u @ w_in
    idx = np.cumsum([A_Q_W, A_KV_W, A_KV_W, B_Q_W, B_KV_W, B_KV_W, D_MODEL]).tolist()
    qa, ka, va, qb, kb, vb, ga, gb = jnp.split(z, idx, axis=-1)

    def heads(t, h):
        return t.reshape(B, S, h, HEAD_DIM)

    oa = window_attention(apply_rotary(heads(qa, A_Q_HEADS), cos1, sin1),
                          apply_rotary(heads(ka, A_KV_HEADS), cos1, sin1),
                          heads(va, A_KV_HEADS), sink_a)
    qbh = apply_axial_rotary(rms_norm(heads(qb, B_Q_HEADS), q_norm_b), cos_r, sin_r, cos_c, sin_c)
    kbh = apply_axial_rotary(rms_norm(heads(kb, B_KV_HEADS), k_norm_b), cos_r, sin_r, cos_c, sin_c)
    ob = dense_attention(qbh, kbh, heads(vb, B_KV_HEADS))

    merged = jax.nn.sigmoid(ga) * (oa @ w_branch_a) + jax.nn.sigmoid(gb) * (ob @ w_branch_b)
    y = merged @ w_out
    x = x + gate_m * rms_norm(y, g_post_mix)

    u = rms_norm(x, g_pre_ffn) * (1 + scale_f) + shift_f
    h1, h3 = jnp.split(u @ w_13, 2, axis=-1)
    y = (jax.nn.silu(h1) * h3) @ w_2
    return x + gate_f * rms_norm(y, g_post_ffn)


def encoder(x, c, g_pre_mix, g_post_mix, g_pre_ffn, g_post_ffn, w_mod, b_mod, w_in, q_norm_b, k_norm_b,
            sink_a, w_branch_a, w_branch_b, w_out, w_13, w_2):
    S = x.shape[1]
    rows = S // GRID_W
    t = jnp.arange(S)
    row = jnp.repeat(jnp.arange(rows), GRID_W)
    col = jnp.tile(jnp.arange(GRID_W), rows)
    rope1d = rope_angles(t, HEAD_DIM)
    cos_r, sin_r = rope_angles(row, HEAD_DIM // 2)
    cos_c, sin_c = rope_angles(col, HEAD_DIM // 2)
    rope_axial = (cos_r, sin_r, cos_c, sin_c)
    for l in range(DEPTH):
        x = encoder_layer(x, c, rope1d, rope_axial, g_pre_mix[l], g_post_mix[l], g_pre_ffn[l], g_post_ffn[l],
                          w_mod[l], b_mod[l], w_in[l], q_norm_b[l], k_norm_b[l], sink_a[l],
                          w_branch_a[l], w_branch_b[l], w_out[l], w_13[l], w_2[l])
    return x


def setup_inputs(seed: int = 0) -> dict:
    key = jax.random.key(seed)
    ks = jax.random.split(key, 20)

    def nrm(k, shape, std):
        return jax.random.normal(k, shape, jnp.float32) * std

    def gain(k, shape):
        return 1.0 + 0.05 * jax.random.normal(k, shape, jnp.float32)

    return {
        "x_prompt": nrm(ks[0], (BATCH, SEQ, D_MODEL), 1.0),
        "x_sample": nrm(ks[1], (DEC_BATCH, DEC_SEQ, D_MODEL), 1.0),
        "c_prompt": nrm(ks[2], (BATCH, D_MODEL), 1.0),
        "c_sample": nrm(ks[3], (DEC_BATCH, D_MODEL), 1.0),
        "g_pre_mix": gain(ks[4], (DEPTH, D_MODEL)),
        "g_post_mix": gain(ks[5], (DEPTH, D_MODEL)),
        "g_pre_ffn": gain(ks[6], (DEPTH, D_MODEL)),
        "g_post_ffn": gain(ks[7], (DEPTH, D_MODEL)),
        "w_mod": nrm(ks[8], (DEPTH, D_MODEL, N_MOD * D_MODEL), 0.5 * D_MODEL ** -0.5),
        "b_mod": nrm(ks[9], (DEPTH, N_MOD * D_MODEL), 0.02),
        "w_in": nrm(ks[10], (DEPTH, D_MODEL, IN_W), D_MODEL ** -0.5),
        "q_norm_b": gain(ks[11], (DEPTH, HEAD_DIM)),
        "k_norm_b": gain(ks[12], (DEPTH, HEAD_DIM)),
        "sink_a": nrm(ks[13], (DEPTH, A_Q_HEADS), 0.5),
        "w_branch_a": nrm(ks[14], (DEPTH, A_Q_W, D_MODEL), A_Q_W ** -0.5),
        "w_branch_b": nrm(ks[15], (DEPTH, B_Q_W, D_MODEL), B_Q_W ** -0.5),
        "w_out": nrm(ks[16], (DEPTH, D_MODEL, D_MODEL), D_MODEL ** -0.5),
        "w_13": nrm(ks[17], (DEPTH, D_MODEL, 2 * D_FF), D_MODEL ** -0.5),
        "w_2": nrm(ks[18], (DEPTH, D_FF, D_MODEL), D_FF ** -0.5),
    }


def reference(x_prompt, x_sample, c_prompt, c_sample, g_pre_mix, g_post_mix, g_pre_ffn, g_post_ffn, w_mod, b_mod,
              w_in, q_norm_b, k_norm_b, sink_a, w_branch_a, w_branch_b, w_out, w_13, w_2):
    y_prompt = encoder(x_prompt, c_prompt, g_pre_mix, g_post_mix, g_pre_ffn, g_post_ffn, w_mod, b_mod, w_in,
                       q_norm_b, k_norm_b, sink_a, w_branch_a, w_branch_b, w_out, w_13, w_2)
    y_sample = encoder(x_sample, c_sample, g_pre_mix, g_post_mix, g_pre_ffn, g_post_ffn, w_mod, b_mod, w_in,
                       q_norm_b, k_norm_b, sink_a, w_branch_a, w_branch_b, w_out, w_13, w_2)
    return (y_prompt, y_sample)
```

```python
import contextlib
import numpy as np
import concourse.bass as bass
import concourse.mybir as mybir
from concourse.bass_utils import run_bass_kernel_spmd

F32 = mybir.dt.float32
BF16 = mybir.dt.bfloat16
ALU = mybir.AluOpType
AF = mybir.ActivationFunctionType

NCORES = 8
D = 2048
KC = D // 128
HD = 128
INW = 7168
DFF = 5632
FC = DFF // 128
GRID_W = 64
THETA = 10000.0
EPS = 1e-6
SCL = float(HD) ** -0.5

CFG = dict(NS=4096, NP=4096, DEPTH=4)


class Buf:
    __slots__ = ("w", "rs", "x")

    def __init__(self, x=False):
        self.w = None
        self.rs = {}
        self.x = x


def bufs(n):
    return [Buf() for _ in range(n)]


class Op:
    __slots__ = ("eng", "fn", "deps", "dma", "sem", "count", "need", "phase", "inc")


DBG = None


class Tracker:
    ENG = ("pe", "act", "dve", "pool", "sp")

    def __init__(self, nc, es):
        self.nc = nc
        self.es = es
        self.sem = {e: es.enter_context(nc.semaphore("s_" + e)) for e in ("pe", "act", "dve", "pool")}
        self.cnt = {e: 0 for e in self.sem}
        self.chan = {}
        self.chcnt = {}
        self.chlast = {}
        self.ops = {e: [] for e in self.ENG}
        self.waited = {e: {} for e in self.ENG}
        self.phase = 0
        self.last = {e: None for e in self.ENG}
        self.pending = {e: [] for e in self.ENG}
        self.nops = 0

    def _chan(self, name):
        if name not in self.chan:
            self.chan[name] = self.es.enter_context(self.nc.semaphore("c_" + name))
            self.chcnt[name] = 0
        return self.chan[name]

    def _mk(self, eng, fn, reads, writes):
        op = Op()
        op.eng = eng
        op.fn = fn
        op.dma = False
        op.need = False
        op.phase = self.phase
        op.count = None
        op.sem = None
        op.inc = 1
        if any(b.x for b in reads):
            writes = list(writes) + [b for b in reads if b.x]
            reads = [b for b in reads if not b.x]
        op_rw = (reads, writes)
        deps = []
        for b in reads:
            if b.w is not None:
                deps.append(b.w)
        for b in writes:
            if b.w is not None:
                deps.append(b.w)
            deps.extend(b.rs.values())
        out = []
        seen = set()
        if self.pending[eng]:
            for d in self.pending[eng]:
                seen.add(id(d))
                out.append(d)
            self.pending[eng] = []
        for d in deps:
            if d is op or id(d) in seen:
                continue
            seen.add(id(d))
            if not d.dma:
                if d.phase != op.phase:
                    continue
                if d.eng == eng and eng == "pe":
                    continue
                d.need = True
            out.append(d)
        op.deps = out
        self._rw = op_rw
        self.ops[eng].append(op)
        self.nops += 1
        return op

    def _reg(self, op, reads, writes):
        key = ("c", id(op.sem)) if op.dma else op.eng
        for b in writes:
            b.w = op
            b.rs = {}
        for b in reads:
            b.rs[key] = op

    def op(self, eng, fn, reads=(), writes=()):
        op = self._mk(eng, fn, reads, writes)
        reads, writes = self._rw
        self._reg(op, reads, writes)
        self.last[eng] = op
        return op

    def dma(self, eng, chan, fn, reads=(), writes=(), inc=16):
        sem = self._chan(chan)
        op = self._mk(eng, fn, reads, writes)
        op.dma = True
        op.sem = sem
        op.inc = inc
        self.chcnt[chan] += inc
        op.count = self.chcnt[chan]
        self.chlast[chan] = op
        reads, writes = self._rw
        self._reg(op, reads, writes)
        return op

    def emit(self):
        nc = self.nc
        for e in ("pe", "act", "dve", "pool"):
            for op in self.ops[e]:
                if not op.dma and op.need:
                    self.cnt[e] += 1
                    op.count = self.cnt[e]
                    op.sem = self.sem[e]
        ends = self.barrier_ops()
        hw = {"pe": "tensor", "act": "scalar", "dve": "vector", "pool": "gpsimd", "sp": "sync"}
        with nc.Block() as block:
            for e in self.ENG:
                ops = self.ops[e]
                waited = self.waited[e]

                def body(eng, ops=ops, waited=waited):
                    for op in ops:
                        wv = {}
                        for d in op.deps:
                            key = id(d.sem)
                            if key not in wv or wv[key][1] < d.count:
                                wv[key] = (d.sem, d.count)
                        for key, (sm, c) in wv.items():
                            if waited.get(key, 0) < c:
                                eng.wait_ge(sm, c)
                                waited[key] = c
                                if DBG is not None:
                                    DBG.write("%s WAIT %s >= %d\n" % (op.eng, getattr(sm, "name", sm), c))
                        ins = op.fn(eng)
                        if DBG is not None:
                            DBG.write("%s OP %s%s\n" % (op.eng, ins.concise()[:150].replace("\n", " "), (" INC %s -> %s" % (getattr(op.sem, "name", op.sem), op.count)) if (op.dma or op.need) else ""))
                        if op.dma:
                            ins.then_inc(op.sem, op.inc)
                        elif op.need:
                            ins.then_inc(op.sem, 1)

                getattr(block, hw[e])(body)
        self.ops = {e: [] for e in self.ENG}
        self.phase += 1
        for e in self.ENG:
            self.pending[e] = list(ends)
        if (self.phase - 1) % 3 == 0 and self.phase < 12:
            self.sem = {e: self.es.enter_context(self.nc.semaphore("s%d_%s" % (self.phase, e))) for e in ("pe", "act", "dve", "pool")}
            self.cnt = {e: 0 for e in self.sem}

    def barrier_ops(self):
        ends = []
        for e in ("pe", "act", "dve", "pool"):
            lo = None
            for op in reversed(self.ops[e]):
                if not op.dma:
                    lo = op
                    break
            if lo is not None:
                if lo.count is None:
                    self.cnt[e] += 1
                    lo.count = self.cnt[e]
                    lo.sem = self.sem[e]
                    lo.need = True
                ends.append(lo)
        ends.extend(self.chlast.values())
        return ends


def build(cfg=CFG):
    NS, NP, DEPTH = cfg["NS"], cfg["NP"], cfg["DEPTH"]
    NT = NS + NP
    T1 = 512
    T2 = 256
    NT1 = NT // T1
    NST1 = NS // T1
    NPT1 = NP // T1
    assert NP == NS
    NB_S = NS // 128
    NB = NT // 128
    NKB_MAX = NB
    NEXT = NB + 2

    nc = bass.Bass("TRN2", target_bir_lowering=False)
    PENG = ("pool", "dve")[cfg.get("peng", 0)]
    STQ = ("pool", "sp")[cfg.get("stq", 0)]

    def din(name, shape, dt=F32):
        return nc.dram_tensor(name, shape, dt, kind="ExternalInput")

    xT = din("xT", [D, NT])
    gT = din("gT", [128, 4 * DEPTH * KC])
    qkg = din("qkg", [128, DEPTH * 2])
    sinkb = din("sinkb", [128, DEPTH * 8])
    ropeT = din("ropeT", [4, 128, NT])
    masks = din("masks", [128, 4 * 512])
    perms = din("perms", [128, 256])
    WSH = {"in": (D, INW), "a": (1024, D), "b": (1024, D), "out": (D, D), "13": (D, 2 * DFF), "2": (DFF, D)}
    w_sh = {k: din("w_" + k, [DEPTH * r, c]) for k, (r, c) in WSH.items()}
    w_mod = din("w_mod", [DEPTH, D, 6 * D])
    bmodT = din("bmodT", [128, DEPTH * 96])
    c2T = din("c2T", [128, KC * 2])
    biasm = din("biasm", [128, 4])
    yT = nc.dram_tensor("yT", [D, NT], F32, kind="ExternalOutput")

    w_bf = {k: nc.dram_tensor("wb_" + k, [DEPTH * r, c], BF16) for k, (r, c) in WSH.items()}
    wb_in, wb_a, wb_b, wb_out, wb_13, wb_2 = [
        [w_bf[k][l * WSH[k][0]:(l + 1) * WSH[k][0], :] for l in range(DEPTH)] for k in ("in", "a", "b", "out", "13", "2")]
    XS = nc.dram_tensor("XS", [D, NT], F32)
    QA = nc.dram_tensor("QA", [128, 8, NT], BF16)
    QB = nc.dram_tensor("QB", [128, 8, NT], BF16)
    GT = nc.dram_tensor("GT", [128, 32, NT], BF16)
    KAx = nc.dram_tensor("KAx", [128, 2, NEXT * 128], BF16)
    VAx = nc.dram_tensor("VAx", [NEXT, 128, 256], BF16)
    KBd = nc.dram_tensor("KBd", [128, 2, NT], BF16)
    VBd = nc.dram_tensor("VBd", [NB, 128, 256], BF16)

    es = contextlib.ExitStack()
    with es:
        T = Tracker(nc, es)

        uniq = [0]

        def sb(name, shape, dt, st=es):
            uniq[0] += 1
            return st.enter_context(nc.sbuf_tensor("%s_u%d" % (name, uniq[0]), shape, dt))

        banks = [es.enter_context(nc.psum_tensor("bank%d" % i, [128, 512], F32)) for i in range(8)]
        bankB = [Buf(x=True) for _ in range(8)]

        ones_d = sb("ones_d", [128, 128], BF16)
        ones_h = sb("ones_h", [128, 128], BF16)
        ones_1 = sb("ones_1", [128, 128], BF16)
        perm64 = sb("perm64", [128, 128], BF16)
        perm32 = sb("perm32", [128, 128], BF16)
        maskb = sb("maskb", [128, 4, 512], BF16)
        bsm = sb("bsm", [128, 4], F32)
        modT = sb("modT", [128, DEPTH, 96, 2], F32)
        AmT = sb("AmT", [128, DEPTH, 2, KC], F32)
        GGm = sb("GGm", [128, DEPTH, 2, KC], F32)
        AfT = sb("AfT", [128, DEPTH, 2, KC], F32)
        GGf = sb("GGf", [128, DEPTH, 2, KC], F32)
        qkgs = sb("qkgs", [128, DEPTH * 2], F32)
        sinke = sb("sinke", [128, DEPTH * 8], F32)
        epsb = sb("epsb", [128, 1], F32)
        B_const = Buf()

        B_xs = [bufs(KC) for _ in range(NT // T2)]
        B_q = bufs(NT1)
        B_g = bufs(NT1)
        B_kax = bufs(NEXT)
        B_vax = bufs(NEXT)
        B_kb = bufs(NT1)
        B_vb = bufs(NT1)
        B_w = {}

        with contextlib.ExitStack() as ps:
            cst = sb("cst", [128, KC * 2], F32, ps)
            gts = sb("gts", [128, 4 * DEPTH * KC], F32, ps)
            bms = sb("bms", [128, DEPTH * 96], F32, ps)
            mks = sb("mks", [128, 4 * 512], F32, ps)
            pms = sb("pms", [128, 256], F32, ps)
            scf = sb("scf", [128, KC * 2], F32, ps)
            B_ld = Buf()
            for dst_, src_ in ((cst, c2T), (gts, gT), (bms, bmodT), (mks, masks), (pms, perms), (bsm, biasm),
                               (qkgs, qkg), (sinke, sinkb)):
                T.dma("sp", "cld", lambda e, d=dst_, s=src_: e.dma_start(out=d[:, :], in_=s[:, :]), writes=[B_ld])

            ci = 0
            for l in range(DEPTH):
                for key in ("in", "a", "b", "out", "13", "2"):
                    r_, c_ = WSH[key]
                    n = r_ * c_ // 2048
                    s2 = w_sh[key][l * r_:(l + 1) * r_, :].rearrange("r c -> (r c)").rearrange("(n f) -> n f", f=2048)
                    d2 = w_bf[key][l * r_:(l + 1) * r_, :].rearrange("r c -> (r c)").rearrange("(n f) -> n f", f=2048)
                    bl = Buf()
                    B_w[(key, l)] = bl
                    for i0 in range(0, n, 1024):
                        i1 = min(n, i0 + 1024)
                        T.dma("pool", "cast%d" % (ci % 4),
                              lambda e, d2=d2, s2=s2, i0=i0, i1=i1: e.dma_start(out=d2[i0:i1, :], in_=s2[i0:i1, :]),
                              writes=[bl])
                        ci += 1

            def memset(t, ap, v):
                T.op("dve", lambda e, ap=ap, v=v: e.memset(ap, v), writes=[B_const])

            memset(None, ones_d[:, :], 1.0 / D)
            memset(None, ones_h[:, :], 1.0 / HD)
            memset(None, ones_1[:, :], 1.0)
            memset(None, epsb[:, :], EPS)
            T.op("dve", lambda e: e.tensor_copy(out=perm64[:, :], in_=pms[:, 0:128]), reads=[B_ld], writes=[B_const])
            T.op("dve", lambda e: e.tensor_copy(out=perm32[:, :], in_=pms[:, 128:256]), reads=[B_ld], writes=[B_const])
            T.op("dve", lambda e: e.tensor_copy(out=maskb[:, :, :].rearrange("p a b -> p (a b)"), in_=mks[:, :]),
                 reads=[B_ld], writes=[B_const])
            zt = sb("zt", [128, 256], BF16, ps)
            B_zt = Buf()
            T.op("dve", lambda e: e.memset(zt[:, :], 0.0), writes=[B_zt])
            for eb in (0, NB + 1):
                T.dma("sp", "zst", lambda e, eb=eb: e.dma_start(
                    out=KAx[:, :, eb * 128:(eb + 1) * 128], in_=zt[:, :].rearrange("p (h t) -> p h t", h=2)),
                    reads=[B_zt], writes=[B_kax[eb]])
                T.dma("sp", "zst", lambda e, eb=eb: e.dma_start(out=VAx[eb], in_=zt[:, :]), reads=[B_zt], writes=[B_vax[eb]])
            T.op("act", lambda e: e.activation(out=sinke[:, :], in_=sinke[:, :], func=AF.Exp), reads=[B_ld], writes=[B_ld])
            T.op("act", lambda e: e.activation(out=scf[:, :], in_=cst[:, :], func=AF.Silu), reads=[B_ld], writes=[B_ld])
            scf3 = scf[:, :].rearrange("p (k s) -> p k s", s=2)

            MB = 768
            wm = [sb("wm%d" % i, [128, KC, MB], F32, ps) for i in range(2)]
            B_wm = bufs(2)
            nblk = 6 * D // MB
            it = 0
            for l in range(DEPTH):
                for bi in range(nblk):
                    s_ = it % 2
                    src = w_mod[l].rearrange("(k p) n -> p k n", p=128)
                    for kk in range(0, KC, 4):
                        T.dma("sp", "wm%d" % s_,
                              lambda e, s_=s_, src=src, bi=bi, kk=kk: e.dma_start(
                                  out=wm[s_][:, kk:kk + 4, :], in_=src[:, kk:kk + 4, bi * MB:(bi + 1) * MB]),
                              writes=[B_wm[s_]])
                    bk = it % 2
                    for jj in range(MB // 128):
                        for k in range(KC):
                            T.op("pe", lambda e, s_=s_, jj=jj, k=k, bk=bk: e.matmul(
                                banks[bk][:, jj * 2:jj * 2 + 2], lhsT=wm[s_][:, k, jj * 128:(jj + 1) * 128],
                                rhs=scf3[:, k, :], start=(k == 0), stop=(k == KC - 1)),
                                reads=[B_wm[s_], B_ld], writes=[bankB[bk]])
                    j0 = bi * (MB // 128)
                    for sg in range(2):
                        T.op("dve", lambda e, l=l, j0=j0, sg=sg, bk=bk: e.tensor_tensor(
                            out=modT[:, l, j0:j0 + MB // 128, sg],
                            in0=banks[bk][:, 0:2 * (MB // 128)].rearrange("p (j s) -> p j s", s=2)[:, :, sg],
                            in1=bms[:, l * 96 + j0:l * 96 + j0 + MB // 128], op=ALU.add),
                            reads=[bankB[bk], B_ld], writes=[B_const])
                    it += 1
            gts4 = gts[:, :].rearrange("p (a l k) -> p a l k", a=4, l=DEPTH)
            for l in range(DEPTH):
                for sg in range(2):
                    def mk(dst, gk, sc_j, mode, l=l, sg=sg):
                        if mode == "A":
                            T.op("dve", lambda e: e.scalar_tensor_tensor(
                                out=dst[:, l, sg, :], in0=modT[:, l, sc_j:sc_j + KC, sg], scalar=1.0,
                                in1=gts4[:, gk, l, :], op0=ALU.add, op1=ALU.mult),
                                reads=[B_const, B_ld], writes=[B_const])
                        else:
                            T.op("dve", lambda e: e.tensor_tensor(
                                out=dst[:, l, sg, :], in0=modT[:, l, sc_j:sc_j + KC, sg],
                                in1=gts4[:, gk, l, :], op=ALU.mult),
                                reads=[B_const, B_ld], writes=[B_const])
                    mk(AmT, 0, 16, "A")
                    mk(GGm, 1, 32, "G")
                    mk(AfT, 2, 64, "A")
                    mk(GGf, 3, 80, "G")
            T.emit()

        def seg_of_tok(c0):
            return 0 if c0 < NS else 1

        def rsqrt_op(out_ap, in_ap, rd, wb):
            T.op("act", lambda e: e.activation(out=out_ap, in_=in_ap, func=AF.Sqrt, bias=epsb[:, 0:1]),
                 reads=rd + [B_const], writes=[wb])
            T.op("dve", lambda e: e.reciprocal(out=out_ap, in_=out_ap), reads=[wb], writes=[wb])

        def norm_mod(xt, B_x, uT, B_u, sq, B_sq, rstd, B_rstd, tmpf, B_tmpf, ssbank, l, sg, A, Bj0, W):
            for k in range(KC):
                s = k % len(sq)
                T.op("act", lambda e, k=k, s=s: e.activation(out=sq[s][:, :W], in_=xt[:, k, :W], func=AF.Square),
                     reads=[B_x[k]], writes=[B_sq[s]])
                T.op("pe", lambda e, k=k, s=s: e.matmul(banks[ssbank][:, :W], lhsT=ones_d[:, :], rhs=sq[s][:, :W],
                                                        start=(k == 0), stop=(k == KC - 1)),
                     reads=[B_sq[s], B_const], writes=[bankB[ssbank]])
            rsqrt_op(rstd[:, :W], banks[ssbank][:, :W], [bankB[ssbank]], B_rstd)
            for k in range(KC):
                s = k % len(tmpf)
                T.op("dve", lambda e, k=k, s=s: e.tensor_tensor(out=tmpf[s][:, :W], in0=xt[:, k, :W], in1=rstd[:, :W],
                                                                op=ALU.mult),
                     reads=[B_x[k], B_rstd], writes=[B_tmpf[s]])
                T.op("act", lambda e, k=k, s=s: e.activation(
                    out=uT[:, k, :W], in_=tmpf[s][:, :W], func=AF.Identity,
                    scale=A[:, l, sg, k:k + 1], bias=modT[:, l, Bj0 + k, sg:sg + 1]),
                    reads=[B_tmpf[s], B_const], writes=[B_u[k]])

        STOP = cfg.get("stop", 99)
        for l in range(DEPTH):
            if STOP == 0:
                break
            Xsrc = xT if l == 0 else XS
            Xdst = yT if l == DEPTH - 1 else XS
            par = l % 2
            with contextlib.ExitStack() as ps:
                xt = sb("p1_x", [128, KC, T1], F32, ps)
                uT = [sb("p1_u%d" % i, [128, KC, T1], BF16, ps) for i in range(2)]
                wr = [sb("p1_w%d" % i, [128, KC, 512], BF16, ps) for i in range(4)]
                stg = [sb("p1_s%d" % i, [128, 4, T1], BF16, ps) for i in range(3)]
                vst = [sb("p1_v%d" % i, [128, 4, 256], BF16, ps) for i in range(2)]
                rope = [sb("p1_r%d" % i, [128, 4, T1], F32, ps) for i in range(2)]
                sq = [sb("p1_sq%d" % i, [128, T1], BF16, ps) for i in range(3)]
                tmpf = [sb("p1_t%d" % i, [128, T1], F32, ps) for i in range(4)]
                rstd = sb("p1_rstd", [128, T1], F32, ps)
                rsth = [sb("p1_rh%d" % i, [128, T1], F32, ps) for i in range(2)]
                qsb = [sb("p1_qs%d" % i, [128, T1], BF16, ps) for i in range(2)]
                qn = [sb("p1_qn%d" % i, [128, T1], F32, ps) for i in range(2)]
                B_x = bufs(KC)
                B_u = [bufs(KC) for _ in range(2)]
                B_wr = bufs(4)
                B_stg = bufs(3)
                B_vst = bufs(2)
                B_rope = bufs(2)
                B_sq = bufs(3)
                B_tmpf = bufs(4)
                B_rstd = Buf()
                B_rsth = bufs(2)
                B_qsb = bufs(2)
                B_qn = bufs(2)
                cnt = dict(w=0, stg=0, vst=0, z=0, aux=0, t=0, q=0)
                Wl = wb_in[l].rearrange("(k p) n -> p k n", p=128)

                order = list(range(NT1))

                def prefetch(ti):
                    t = order[ti]
                    c0 = t * T1
                    sg = 0 if t < NST1 else 1
                    ub = ti % 2
                    rb = ti % 2
                    T.dma("sp", "p1x", lambda e, c0=c0: e.dma_start(
                        out=xt[:, :, :], in_=Xsrc.ap().rearrange("(k p) t -> p k t", p=128)[:, :, c0:c0 + T1]),
                        reads=B_xs[c0 // T2] + B_xs[c0 // T2 + 1], writes=B_x)
                    T.dma("sp", "p1r%d" % rb, lambda e, c0=c0, rb=rb: e.dma_start(
                        out=rope[rb][:, :, :], in_=ropeT.ap().rearrange("f p t -> p f t")[:, :, c0:c0 + T1]),
                        writes=[B_rope[rb]])
                    norm_mod(xt, B_x, uT[ub], B_u[ub], sq, B_sq, rstd, B_rstd, tmpf, B_tmpf, 4, l, sg, AmT, 0, T1)

                order = order[:cfg.get("p1_tiles", 99)]
                prefetch(0)
                for ti, t in enumerate(order):
                    c0 = t * T1
                    sg = 0 if t < NST1 else 1
                    tp = t - NST1
                    ub = ti % 2
                    rb = ti % 2

                    for blk in range(cfg.get("p1_blks", 14)):
                        if blk == 9 and ti + 1 < len(order):
                            prefetch(ti + 1)
                        ws = cnt["w"] % 4
                        cnt["w"] += 1
                        T.dma("sp", "p1w%d" % ws, lambda e, ws=ws, blk=blk: e.dma_start(
                            out=wr[ws][:, :, :], in_=Wl[:, :, blk * 512:(blk + 1) * 512]),
                            reads=[B_w[("in", l)]], writes=[B_wr[ws]])
                        kind = ("qa", "qa", "kva", "qb", "qb", "kvb", "g", "g", "g", "g", "g", "g", "g", "g")[blk]
                        nfm = 2 if kind in ("kva", "kvb") else 4
                        ss_ = cnt["stg"] % 3
                        cnt["stg"] += 1
                        for j in range(nfm):
                            zb = cnt["z"] % 4
                            cnt["z"] += 1
                            for rep in range(cfg.get("zrep", 1)):
                              for k in range(KC):
                                T.op("pe", lambda e, ws=ws, j=j, k=k, zb=zb, ub=ub, rep=rep: e.matmul(
                                    banks[zb][:, :], lhsT=wr[ws][:, k, j * 128:(j + 1) * 128], rhs=uT[ub][:, k, :],
                                    start=(k == 0 and rep == 0), stop=(k == KC - 1)),
                                    reads=[B_wr[ws], B_u[ub][k]], writes=[bankB[zb]])
                            if cfg.get("p1_epi", 2) == 0:
                                continue
                            if kind == "g":
                                T.op("act", lambda e, zb=zb, ss_=ss_, j=j: e.activation(
                                    out=stg[ss_][:, j, :], in_=banks[zb][:, :], func=AF.Sigmoid),
                                    reads=[bankB[zb]], writes=[B_stg[ss_]])
                                continue
                            isb = kind in ("qb", "kvb")
                            a = cnt["aux"] % 2
                            cnt["aux"] += 1
                            ab = cfg.get("abase", 5) + a
                            if isb:
                                gcol = l * 2 + (0 if kind == "qb" else 1)
                                q_ = cnt["q"] % 3
                                cnt["q"] += 1
                                T.op("act", lambda e, zb=zb, q_=q_: e.activation(out=sq[q_][:, :], in_=banks[zb][:, :], func=AF.Square),
                                     reads=[bankB[zb]], writes=[B_sq[q_]])
                                T.op("pe", lambda e, q_=q_, ab=ab: e.matmul(banks[ab][:, :], lhsT=ones_h[:, :], rhs=sq[q_][:, :],
                                                                            start=True, stop=True),
                                     reads=[B_sq[q_], B_const], writes=[bankB[ab]])
                                rsqrt_op(rsth[a][:, :], banks[ab][:, :], [bankB[ab]], B_rsth[a])
                                T.op("dve", lambda e, a=a, zb=zb, gcol=gcol: e.scalar_tensor_tensor(
                                    out=qn[a][:, :], in0=banks[zb][:, :], scalar=qkgs[:, gcol:gcol + 1], in1=rsth[a][:, :],
                                    op0=ALU.mult, op1=ALU.mult), reads=[bankB[zb], B_rsth[a], B_ld], writes=[B_qn[a]])
                                T.op("act", lambda e, a=a: e.activation(out=qsb[a][:, :], in_=qn[a][:, :], func=AF.Identity),
                                     reads=[B_qn[a]], writes=[B_qsb[a]])
                                src_ap = qn[a][:, :]
                                srcB = B_qn[a]
                                pm = perm32
                                ci, si = 2, 3
                            else:
                                T.op("act", lambda e, a=a, zb=zb: e.activation(out=qsb[a][:, :], in_=banks[zb][:, :], func=AF.Identity),
                                     reads=[bankB[zb]], writes=[B_qsb[a]])
                                src_ap = banks[zb][:, :]
                                srcB = bankB[zb]
                                pm = perm64
                                ci, si = 0, 1
                            EL = cfg.get("epi_lvl", 9)
                            if EL < 2:
                                continue
                            if cfg.get("dbg_pm", 0) == 1:
                                pm = ones_h
                            if cfg.get("dbg_pm", 0) == 2:
                                T.op("pe", lambda e, a=a, ab=ab, pm=pm, ub=ub: e.matmul(banks[ab][:, :], lhsT=pm[:, :], rhs=uT[ub][:, 0, :],
                                                                                 start=True, stop=True),
                                     reads=[B_u[ub][0], B_const], writes=[bankB[ab]])
                                continue
                            T.op("pe", lambda e, a=a, ab=ab, pm=pm: e.matmul(banks[ab][:, :], lhsT=pm[:, :], rhs=qsb[a][:, :],
                                                                             start=True, stop=True),
                                 reads=[B_qsb[a], B_const], writes=[bankB[ab]])
                            if EL < 3:
                                continue
                            t1 = cnt["t"] % 4
                            t2 = (cnt["t"] + 1) % 4
                            cnt["t"] += 2
                            T.op("dve", lambda e, t1=t1, src_ap=src_ap, rb=rb, ci=ci: e.tensor_tensor(
                                out=tmpf[t1][:, :], in0=src_ap, in1=rope[rb][:, ci, :], op=ALU.mult),
                                reads=[srcB, B_rope[rb]], writes=[B_tmpf[t1]])
                            T.op("dve", lambda e, t2=t2, ab=ab, rb=rb, si=si: e.tensor_tensor(
                                out=tmpf[t2][:, :], in0=banks[ab][:, :], in1=rope[rb][:, si, :], op=ALU.mult),
                                reads=[bankB[ab], B_rope[rb]], writes=[B_tmpf[t2]])
                            if EL < 4:
                                continue
                            T.op(PENG, lambda e, t1=t1, t2=t2, ss_=ss_, j=j: e.tensor_tensor(
                                out=stg[ss_][:, j, :], in0=tmpf[t1][:, :], in1=tmpf[t2][:, :], op=ALU.add),
                                reads=[B_tmpf[t1], B_tmpf[t2]], writes=[B_stg[ss_]])
                        if kind in ("kva", "kvb"):
                            vs = cnt["vst"] % 2
                            cnt["vst"] += 1
                            for s4 in range(4):
                                zb = cnt["z"] % 4
                                cnt["z"] += 1
                                for k in range(KC):
                                    T.op("pe", lambda e, ws=ws, s4=s4, k=k, zb=zb, ub=ub: e.matmul(
                                        banks[zb][:, 0:256], lhsT=uT[ub][:, k, s4 * 128:(s4 + 1) * 128], rhs=wr[ws][:, k, 256:512],
                                        start=(k == 0), stop=(k == KC - 1)),
                                        reads=[B_wr[ws], B_u[ub][k]], writes=[bankB[zb]])
                                T.op("dve", lambda e, zb=zb, vs=vs, s4=s4: e.tensor_copy(out=vst[vs][:, s4, :], in_=banks[zb][:, 0:256]),
                                     reads=[bankB[zb]], writes=[B_vst[vs]])
                        if cfg.get("p1_epi", 2) < 2:
                            continue
                        def st(chan, fn, reads, writes):
                            T.dma(STQ, chan, fn, reads=reads, writes=writes)
                        sch = "p1s%d" % ss_
                        if kind == "qa" or kind == "qb":
                            dst = QA if kind == "qa" else QB
                            h0 = 0 if blk in (0, 3) else 4
                            st(sch, lambda e, dst=dst, h0=h0, ss_=ss_, c0=c0: e.dma_start(
                                out=dst[:, h0:h0 + 4, c0:c0 + T1], in_=stg[ss_][:, :, :]),
                                [B_stg[ss_]], [B_q[t]])
                        elif kind == "g":
                            j0 = (blk - 6) * 4
                            st(sch, lambda e, j0=j0, ss_=ss_, c0=c0: e.dma_start(
                                out=GT[:, j0:j0 + 4, c0:c0 + T1], in_=stg[ss_][:, :, :]),
                                [B_stg[ss_]], [B_g[t]])
                        elif kind == "kva":
                            eb = t * 4 + 1
                            st(sch, lambda e, ss_=ss_, eb=eb: e.dma_start(
                                out=KAx[:, :, eb * 128:(eb + 4) * 128], in_=stg[ss_][:, 0:2, :]),
                                [B_stg[ss_]], B_kax[eb:eb + 4])
                            st("p1v%d" % vs, lambda e, vs=vs, eb=eb: e.dma_start(
                                out=VAx[eb:eb + 4].rearrange("b p c -> p b c"), in_=vst[vs][:, :, :]),
                                [B_vst[vs]], B_vax[eb:eb + 4])
                        elif kind == "kvb":
                            st(sch, lambda e, ss_=ss_, c0=c0: e.dma_start(out=KBd[:, :, c0:c0 + T1], in_=stg[ss_][:, 0:2, :]),
                               [B_stg[ss_]], [B_kb[t]])
                            st("p1v%d" % vs, lambda e, vs=vs, t=t: e.dma_start(
                                out=VBd[t * 4:t * 4 + 4].rearrange("b p c -> p b c"), in_=vst[vs][:, :, :]),
                                [B_vst[vs]], [B_vb[t]])
                T.emit()
            if STOP == 1:
                break

            with contextlib.ExitStack() as ps:
                KBs = sb("a_kb", [128, 2, NKB_MAX * 128], BF16, ps)
                VBs = sb("a_vb", [128, NKB_MAX, 256], BF16, ps)
                qa_s = [sb("a_qa%d" % i, [128, 8, T2], BF16, ps) for i in range(2)]
                qb_s = [sb("a_qb%d" % i, [128, 8, T2], BF16, ps) for i in range(2)]
                gt_s = [sb("a_g%d" % i, [128, 8, T2], BF16, ps) for i in range(2)]
                kw = [sb("a_kw%d" % i, [128, 2, 512], BF16, ps) for i in range(2)]
                vw = [sb("a_vw%d" % i, [128, 4, 256], BF16, ps) for i in range(2)]
                oa = sb("a_oa", [128, 8, T2], BF16, ps)
                ob = sb("a_ob", [128, 8, T2], BF16, ps)
                mg = sb("a_mg", [128, KC, T2], BF16, ps)
                yg = sb("a_yg", [128, KC, T2], F32, ps)
                wr = [sb("a_w%d" % i, [128, KC, 512], BF16, ps) for i in range(2)]
                pb = [sb("a_p%d" % i, [128, 512], BF16, ps) for i in range(4)]
                dn = [sb("a_dn%d" % i, [128, 512], F32, ps) for i in range(2)]
                tf = [sb("a_tf%d" % i, [128, T2], F32, ps) for i in range(4)]
                sq = [sb("a_sq%d" % i, [128, T2], BF16, ps) for i in range(3)]
                xr = [sb("a_x%d" % i, [128, 2, T2], F32, ps) for i in range(2)]
                xo = [sb("a_xo%d" % i, [128, 2, T2], F32, ps) for i in range(2)]
                rstd = sb("a_rstd", [128, T2], F32, ps)
                sexp = sb("a_sexp", [128, 8, 128], F32, ps)
                onesf = sb("a_onesf", [128, 128], F32, ps)
                B_sexp = Buf()
                T.op("dve", lambda e: e.memset(onesf[:, :], 1.0), writes=[B_sexp])
                for hh in range(8):
                    T.op("dve", lambda e, hh=hh: e.tensor_scalar(
                        out=sexp[:, hh, :], in0=onesf[:, :], scalar1=sinke[:, l * 8 + hh:l * 8 + hh + 1], scalar2=None,
                        op0=ALU.mult), reads=[B_sexp], writes=[B_sexp])
                B_KB, B_VB = Buf(), Buf()
                B_qa, B_qb, B_gt = bufs(2), bufs(2), bufs(2)
                B_kw, B_vw = bufs(2), bufs(2)
                B_oa, B_ob = bufs(8), bufs(8)
                B_mg, B_yg = bufs(KC), bufs(KC)
                B_wr = bufs(2)
                B_pb, B_dn, B_tf, B_sq = bufs(4), bufs(2), bufs(4), bufs(3)
                B_xr, B_xo = bufs(2), bufs(2)
                B_rstd = Buf()
                cnt = dict(p=0, s=0, od=0, dn=0, w=0, z=0, tf=0, sq=0, x=0, g=0)
                Wa = wb_a[l].rearrange("(k p) n -> p k n", p=128)
                Wb = wb_b[l].rearrange("(k p) n -> p k n", p=128)
                Wo = wb_out[l].rearrange("(k p) n -> p k n", p=128)
                NT2 = NT // T2
                NST2 = NS // T2
                for tt in range(NT2):
                    c0 = tt * T2
                    sg = 0 if tt < NST2 else 1
                    if tt == 0:
                        for t in range(NT1):
                            T.dma("sp", "akb", lambda e, t=t: e.dma_start(out=KBs[:, :, t * T1:(t + 1) * T1], in_=KBd[:, :, t * T1:(t + 1) * T1]),
                                  reads=[B_kb[t]], writes=[B_KB])
                            T.dma("sp", "avb", lambda e, t=t: e.dma_start(
                                out=VBs[:, t * 4:t * 4 + 4, :], in_=VBd[t * 4:t * 4 + 4].rearrange("b p c -> p b c")),
                                reads=[B_vb[t]], writes=[B_VB])
                    qs = tt % 2
                    t1i = c0 // T1
                    T.dma("sp", "aqa%d" % qs, lambda e, qs=qs, c0=c0: e.dma_start(out=qa_s[qs][:, :, :], in_=QA[:, :, c0:c0 + T2]),
                          reads=[B_q[t1i]], writes=[B_qa[qs]])
                    T.dma("sp", "aqb%d" % qs, lambda e, qs=qs, c0=c0: e.dma_start(out=qb_s[qs][:, :, :], in_=QB[:, :, c0:c0 + T2]),
                          reads=[B_q[t1i]], writes=[B_qb[qs]])
                    b0 = c0 // 128
                    e0 = b0
                    T.dma("sp", "akw%d" % qs, lambda e, qs=qs, e0=e0: e.dma_start(out=kw[qs][:, :, :], in_=KAx[:, :, e0 * 128:(e0 + 4) * 128]),
                          reads=B_kax[e0:e0 + 4], writes=[B_kw[qs]])
                    T.dma("sp", "avw%d" % qs, lambda e, qs=qs, e0=e0: e.dma_start(
                        out=vw[qs][:, :, :], in_=VAx[e0:e0 + 4].rearrange("b p c -> p b c")),
                        reads=B_vax[e0:e0 + 4], writes=[B_vw[qs]])
                    nkb = NB

                    def attn(qsrc, B_qsrc, osb, B_o, window):
                        for h in range(2):
                            for s in range(2):
                                qap = qsrc[:, 4 * h:4 * h + 4, s * 128:(s + 1) * 128]
                                if window:
                                    kbl = []
                                    for d_ in range(3):
                                        lb = b0 + s + d_ - 1
                                        if lb < 0 or lb >= NB:
                                            continue
                                        cross = (lb // NB_S) != ((b0 + s) // NB_S)
                                        mk = None
                                        if d_ == 0:
                                            mk = 2 if cross else 0
                                        if d_ == 2:
                                            mk = 3 if cross else 1
                                        kbl.append((s + d_, mk, None))
                                else:
                                    kbl = [(i, None, 2 * sg + (i // NB_S)) for i in range(nkb)]
                                ob_ = 3 + cnt["od"] % 2
                                db_ = 5 + cnt["od"] % 2
                                cnt["od"] += 1
                                for i, (kb_, mk, bi_) in enumerate(kbl):
                                    sbk = cnt["s"] % 3
                                    cnt["s"] += 1
                                    p_ = cnt["p"] % 4
                                    cnt["p"] += 1
                                    if window:
                                        lhsK = kw[qs][:, h, kb_ * 128:(kb_ + 1) * 128]
                                        lhsV = vw[qs][:, kb_, h * 128:(h + 1) * 128]
                                        rK, rV = [B_kw[qs]], [B_vw[qs]]
                                    else:
                                        lhsK = KBs[:, h, kb_ * 128:(kb_ + 1) * 128]
                                        lhsV = VBs[:, kb_, h * 128:(h + 1) * 128]
                                        rK, rV = [B_KB], [B_VB]
                                    T.op("pe", lambda e, sbk=sbk, lhsK=lhsK, qap=qap: e.matmul(
                                        banks[sbk][:, :].rearrange("p (a b) -> p a b", a=4), lhsT=lhsK, rhs=qap, start=True, stop=True),
                                        reads=rK + [B_qsrc], writes=[bankB[sbk]])
                                    if bi_ is None:
                                        T.op("act", lambda e, sbk=sbk, p_=p_: e.activation(out=pb[p_][:, :], in_=banks[sbk][:, :], func=AF.Exp, scale=SCL),
                                             reads=[bankB[sbk]], writes=[B_pb[p_]])
                                    else:
                                        T.op("act", lambda e, sbk=sbk, p_=p_, bi_=bi_: e.activation(
                                            out=pb[p_][:, :], in_=banks[sbk][:, :], func=AF.Exp, scale=SCL, bias=bsm[:, bi_:bi_ + 1]),
                                            reads=[bankB[sbk], B_const], writes=[B_pb[p_]])
                                    if mk is not None:
                                        T.op(PENG, lambda e, p_=p_, mk=mk: e.tensor_tensor(out=pb[p_][:, :], in0=pb[p_][:, :], in1=maskb[:, mk, :], op=ALU.mult),
                                             reads=[B_pb[p_], B_const], writes=[B_pb[p_]])
                                    first, last = (i == 0), (i == len(kbl) - 1)
                                    T.op("pe", lambda e, ob_=ob_, lhsV=lhsV, p_=p_, first=first, last=last: e.matmul(
                                        banks[ob_][:, :], lhsT=lhsV, rhs=pb[p_][:, :], start=first, stop=last),
                                        reads=rV + [B_pb[p_]], writes=[bankB[ob_]])
                                    T.op("pe", lambda e, db_=db_, p_=p_, first=first, last=last: e.matmul(
                                        banks[db_][:, :], lhsT=ones_1[:, :], rhs=pb[p_][:, :], start=first, stop=last),
                                        reads=[B_const, B_pb[p_]], writes=[bankB[db_]])
                                d_i = cnt["dn"] % 2
                                cnt["dn"] += 1
                                if window:
                                    T.op("dve", lambda e, d_i=d_i, db_=db_, h=h: e.tensor_tensor(
                                        out=dn[d_i][:, :], in0=banks[db_][:, :], in1=sexp[:, 4 * h:4 * h + 4, :].rearrange("p a b -> p (a b)"), op=ALU.add),
                                        reads=[bankB[db_], B_sexp], writes=[B_dn[d_i]])
                                    T.op("dve", lambda e, d_i=d_i: e.reciprocal(out=dn[d_i][:, :], in_=dn[d_i][:, :]),
                                         reads=[B_dn[d_i]], writes=[B_dn[d_i]])
                                else:
                                    T.op("dve", lambda e, d_i=d_i, db_=db_: e.reciprocal(out=dn[d_i][:, :], in_=banks[db_][:, :]),
                                         reads=[bankB[db_]], writes=[B_dn[d_i]])
                                T.op("dve", lambda e, d_i=d_i, ob_=ob_, h=h, s=s: e.tensor_tensor(
                                    out=osb[:, 4 * h:4 * h + 4, s * 128:(s + 1) * 128], in0=banks[ob_][:, :].rearrange("p (a b) -> p a b", a=4),
                                    in1=dn[d_i][:, :].rearrange("p (a b) -> p a b", a=4), op=ALU.mult),
                                    reads=[bankB[ob_], B_dn[d_i]], writes=B_o[4 * h:4 * h + 4])

                    attn(qa_s[qs], B_qa[qs], oa, B_oa, True)
                    attn(qb_s[qs], B_qb[qs], ob, B_ob, False)

                    for blk in range(4):
                        ws = cnt["w"] % 2
                        cnt["w"] += 1
                        T.dma("sp", "aw%d" % ws, lambda e, ws=ws, blk=blk: e.dma_start(out=wr[ws][:, 0:8, :], in_=Wa[:, :, blk * 512:(blk + 1) * 512]),
                              reads=[B_w[("a", l)]], writes=[B_wr[ws]])
                        T.dma("sp", "aw%d" % ws, lambda e, ws=ws, blk=blk: e.dma_start(out=wr[ws][:, 8:16, :], in_=Wb[:, :, blk * 512:(blk + 1) * 512]),
                              reads=[B_w[("b", l)]], writes=[B_wr[ws]])
                        gs = cnt["g"] % 2
                        cnt["g"] += 1
                        T.dma("sp", "ag%d" % gs, lambda e, gs=gs, blk=blk, c0=c0: e.dma_start(out=gt_s[gs][:, 0:4, :], in_=GT[:, blk * 4:blk * 4 + 4, c0:c0 + T2]),
                              reads=[B_g[t1i]], writes=[B_gt[gs]])
                        T.dma("sp", "ag%d" % gs, lambda e, gs=gs, blk=blk, c0=c0: e.dma_start(out=gt_s[gs][:, 4:8, :], in_=GT[:, 16 + blk * 4:16 + blk * 4 + 4, c0:c0 + T2]),
                              reads=[B_g[t1i]], writes=[B_gt[gs]])
                        for j in range(4):
                            za = cnt["z"] % 6
                            zb = (cnt["z"] + 1) % 6
                            cnt["z"] += 2
                            for k in range(8):
                                T.op("pe", lambda e, ws=ws, j=j, k=k, za=za: e.matmul(
                                    banks[za][:, :T2], lhsT=wr[ws][:, k, j * 128:(j + 1) * 128], rhs=oa[:, k, :], start=(k == 0), stop=(k == 7)),
                                    reads=[B_wr[ws], B_oa[k]], writes=[bankB[za]])
                            for k in range(8):
                                T.op("pe", lambda e, ws=ws, j=j, k=k, zb=zb: e.matmul(
                                    banks[zb][:, :T2], lhsT=wr[ws][:, 8 + k, j * 128:(j + 1) * 128], rhs=ob[:, k, :], start=(k == 0), stop=(k == 7)),
                                    reads=[B_wr[ws], B_ob[k]], writes=[bankB[zb]])
                            t1 = cnt["tf"] % 4
                            t2 = (cnt["tf"] + 1) % 4
                            cnt["tf"] += 2
                            T.op("dve", lambda e, t1=t1, za=za, gs=gs, j=j: e.tensor_tensor(out=tf[t1][:, :], in0=banks[za][:, :T2], in1=gt_s[gs][:, j, :], op=ALU.mult),
                                 reads=[bankB[za], B_gt[gs]], writes=[B_tf[t1]])
                            T.op("dve", lambda e, t2=t2, zb=zb, gs=gs, j=j: e.tensor_tensor(out=tf[t2][:, :], in0=banks[zb][:, :T2], in1=gt_s[gs][:, 4 + j, :], op=ALU.mult),
                                 reads=[bankB[zb], B_gt[gs]], writes=[B_tf[t2]])
                            jj = blk * 4 + j
                            T.op(PENG, lambda e, t1=t1, t2=t2, jj=jj: e.tensor_tensor(out=mg[:, jj, :], in0=tf[t1][:, :], in1=tf[t2][:, :], op=ALU.add),
                                 reads=[B_tf[t1], B_tf[t2]], writes=[B_mg[jj]])
                    for blk in range(4):
                        ws = cnt["w"] % 2
                        cnt["w"] += 1
                        T.dma("sp", "aw%d" % ws, lambda e, ws=ws, blk=blk: e.dma_start(out=wr[ws][:, :, :], in_=Wo[:, :, blk * 512:(blk + 1) * 512]),
                              reads=[B_w[("out", l)]], writes=[B_wr[ws]])
                        for j in range(4):
                            za = cnt["z"] % 6
                            cnt["z"] += 1
                            jj = blk * 4 + j
                            for k in range(KC):
                                T.op("pe", lambda e, ws=ws, j=j, k=k, za=za: e.matmul(
                                    banks[za][:, :T2], lhsT=wr[ws][:, k, j * 128:(j + 1) * 128], rhs=mg[:, k, :], start=(k == 0), stop=(k == KC - 1)),
                                    reads=[B_wr[ws], B_mg[k]], writes=[bankB[za]])
                            q_ = cnt["sq"] % 3
                            cnt["sq"] += 1
                            T.op("act", lambda e, q_=q_, za=za: e.activation(out=sq[q_][:, :], in_=banks[za][:, :T2], func=AF.Square),
                                 reads=[bankB[za]], writes=[B_sq[q_]])
                            T.op("dve", lambda e, za=za, jj=jj, sg=sg: e.tensor_scalar(out=yg[:, jj, :], in0=banks[za][:, :T2], scalar1=GGm[:, l, sg, jj:jj + 1],
                                                                               scalar2=None, op0=ALU.mult),
                                 reads=[bankB[za], B_const], writes=[B_yg[jj]])
                            T.op("pe", lambda e, q_=q_, jj=jj: e.matmul(banks[7][:, :T2], lhsT=ones_d[:, :], rhs=sq[q_][:, :], start=(jj == 0), stop=(jj == KC - 1)),
                                 reads=[B_sq[q_], B_const], writes=[bankB[7]])
                    rsqrt_op(rstd[:, :], banks[7][:, :T2], [bankB[7]], B_rstd)
                    for g4 in range(8):
                        xs_ = cnt["x"] % 2
                        cnt["x"] += 1
                        T.dma("sp", "ax%d" % xs_, lambda e, xs_=xs_, g4=g4, c0=c0: e.dma_start(
                            out=xr[xs_][:, :, :], in_=Xsrc.ap().rearrange("(k p) t -> p k t", p=128)[:, g4 * 2:g4 * 2 + 2, c0:c0 + T2]),
                            reads=B_xs[tt][g4 * 2:g4 * 2 + 2], writes=[B_xr[xs_]])
                        for j in range(2):
                            jj = g4 * 2 + j
                            t1 = cnt["tf"] % 4
                            cnt["tf"] += 1
                            T.op("dve", lambda e, t1=t1, jj=jj: e.tensor_tensor(out=tf[t1][:, :], in0=yg[:, jj, :], in1=rstd[:, :], op=ALU.mult),
                                 reads=[B_yg[jj], B_rstd], writes=[B_tf[t1]])
                            T.op(PENG, lambda e, t1=t1, xs_=xs_, j=j: e.tensor_tensor(out=xo[xs_][:, j, :], in0=tf[t1][:, :], in1=xr[xs_][:, j, :], op=ALU.add),
                                 reads=[B_tf[t1], B_xr[xs_]], writes=[B_xo[xs_]])
                        T.dma("pool", "axo%d" % xs_, lambda e, xs_=xs_, g4=g4, c0=c0: e.dma_start(
                            out=XS.ap().rearrange("(k p) t -> p k t", p=128)[:, g4 * 2:g4 * 2 + 2, c0:c0 + T2], in_=xo[xs_][:, :, :]),
                            reads=[B_xo[xs_]], writes=B_xs[tt][g4 * 2:g4 * 2 + 2])
                T.emit()
            if STOP == 2:
                break

            with contextlib.ExitStack() as ps:
                xy = sb("f_xy", [128, KC, T1], F32, ps)
                uT = sb("f_u", [128, KC, T1], BF16, ps)
                hT = sb("f_h", [128, FC, T1], BF16, ps)
                w13r = [sb("f_w13_%d" % i, [128, KC, 512], BF16, ps) for i in range(2)]
                w2r = [sb("f_w2_%d" % i, [128, FC, 256], BF16, ps) for i in range(2)]
                sq = [sb("f_sq%d" % i, [128, T1], BF16, ps) for i in range(2)]
                tmpf = [sb("f_t%d" % i, [128, T1], F32, ps) for i in range(2)]
                rstd = sb("f_rstd", [128, T1], F32, ps)
                sl = [sb("f_sl%d" % i, [128, T1], BF16, ps) for i in range(2)]
                xr = [sb("f_x%d" % i, [128, T1], F32, ps) for i in range(2)]
                xo = [sb("f_xo%d" % i, [128, T1], F32, ps) for i in range(2)]
                B_xy, B_u, B_h = bufs(KC), bufs(KC), bufs(FC)
                B_w13, B_w2 = bufs(2), bufs(2)
                B_sq, B_tmpf, B_sl = bufs(2), bufs(2), bufs(2)
                B_rstd = Buf()
                B_xr, B_xo = bufs(2), bufs(2)
                cnt = dict(w13=0, w2=0, z=0, sq=0, sl=0, x=0, tf=0)
                W13 = wb_13[l].rearrange("(k p) n -> p k n", p=128)
                W2 = wb_2[l].rearrange("(k p) n -> p k n", p=128)
                for t in range(NT1):
                    c0 = t * T1
                    sg = 0 if t < NST1 else 1
                    xb0, xb1 = B_xs[c0 // T2], B_xs[c0 // T2 + 1]
                    T.dma("sp", "fx", lambda e, c0=c0: e.dma_start(
                        out=xy[:, :, :], in_=XS.ap().rearrange("(k p) t -> p k t", p=128)[:, :, c0:c0 + T1]),
                        reads=xb0 + xb1, writes=B_xy)
                    norm_mod(xy, B_xy, uT, B_u, sq, B_sq, rstd, B_rstd, tmpf, B_tmpf, 7, l, sg, AfT, 48, T1)
                    for blk in range(FC // 2):
                        ws = cnt["w13"] % 2
                        cnt["w13"] += 1
                        T.dma("sp", "fw13_%d" % ws, lambda e, ws=ws, blk=blk: e.dma_start(out=w13r[ws][:, :, 0:256], in_=W13[:, :, blk * 256:(blk + 1) * 256]),
                              reads=[B_w[("13", l)]], writes=[B_w13[ws]])
                        T.dma("sp", "fw13_%d" % ws, lambda e, ws=ws, blk=blk: e.dma_start(out=w13r[ws][:, :, 256:512], in_=W13[:, :, DFF + blk * 256:DFF + (blk + 1) * 256]),
                              reads=[B_w[("13", l)]], writes=[B_w13[ws]])
                        for j in range(2):
                            z1 = cnt["z"] % 6
                            z3 = (cnt["z"] + 1) % 6
                            cnt["z"] += 2
                            for k in range(KC):
                                T.op("pe", lambda e, ws=ws, j=j, k=k, z1=z1: e.matmul(
                                    banks[z1][:, :], lhsT=w13r[ws][:, k, j * 128:(j + 1) * 128], rhs=uT[:, k, :], start=(k == 0), stop=(k == KC - 1)),
                                    reads=[B_w13[ws], B_u[k]], writes=[bankB[z1]])
                            for k in range(KC):
                                T.op("pe", lambda e, ws=ws, j=j, k=k, z3=z3: e.matmul(
                                    banks[z3][:, :], lhsT=w13r[ws][:, k, 256 + j * 128:256 + (j + 1) * 128], rhs=uT[:, k, :], start=(k == 0), stop=(k == KC - 1)),
                                    reads=[B_w13[ws], B_u[k]], writes=[bankB[z3]])
                            s_ = cnt["sl"] % 2
                            cnt["sl"] += 1
                            jj = blk * 2 + j
                            T.op("act", lambda e, s_=s_, z1=z1: e.activation(out=sl[s_][:, :], in_=banks[z1][:, :], func=AF.Silu),
                                 reads=[bankB[z1]], writes=[B_sl[s_]])
                            T.op("dve", lambda e, s_=s_, z3=z3, jj=jj: e.tensor_tensor(out=hT[:, jj, :], in0=banks[z3][:, :], in1=sl[s_][:, :], op=ALU.mult),
                                 reads=[bankB[z3], B_sl[s_]], writes=[B_h[jj]])
                    for blk in range(8):
                        ws = cnt["w2"] % 2
                        cnt["w2"] += 1
                        for k0 in range(0, FC, 11):
                            T.dma("sp", "fw2_%d" % ws, lambda e, ws=ws, blk=blk, k0=k0: e.dma_start(
                                out=w2r[ws][:, k0:k0 + 11, :], in_=W2[:, k0:k0 + 11, blk * 256:(blk + 1) * 256]),
                                reads=[B_w[("2", l)]], writes=[B_w2[ws]])
                        for j in range(2):
                            za = cnt["z"] % 6
                            cnt["z"] += 1
                            jj = blk * 2 + j
                            for k in range(FC):
                                T.op("pe", lambda e, ws=ws, j=j, k=k, za=za: e.matmul(
                                    banks[za][:, :], lhsT=w2r[ws][:, k, j * 128:(j + 1) * 128], rhs=hT[:, k, :], start=(k == 0), stop=(k == FC - 1)),
                                    reads=[B_w2[ws], B_h[k]], writes=[bankB[za]])
                            q_ = cnt["sq"] % 2
                            cnt["sq"] += 1
                            T.op("act", lambda e, q_=q_, za=za: e.activation(out=sq[q_][:, :], in_=banks[za][:, :], func=AF.Square),
                                 reads=[bankB[za]], writes=[B_sq[q_]])
                            T.op("dve", lambda e, za=za, jj=jj, sg=sg: e.tensor_scalar(out=xy[:, jj, :], in0=banks[za][:, :], scalar1=GGf[:, l, sg, jj:jj + 1],
                                                                               scalar2=None, op0=ALU.mult),
                                 reads=[bankB[za], B_const], writes=[B_xy[jj]])
                            T.op("pe", lambda e, q_=q_, jj=jj: e.matmul(banks[7][:, :], lhsT=ones_d[:, :], rhs=sq[q_][:, :], start=(jj == 0), stop=(jj == KC - 1)),
                                 reads=[B_sq[q_], B_const], writes=[bankB[7]])
                    rsqrt_op(rstd[:, :], banks[7][:, :], [bankB[7]], B_rstd)
                    for jj in range(KC):
                        xs_ = cnt["x"] % 2
                        cnt["x"] += 1
                        T.dma("sp", "fxr%d" % xs_, lambda e, xs_=xs_, jj=jj, c0=c0: e.dma_start(
                            out=xr[xs_][:, :], in_=XS[jj * 128:(jj + 1) * 128, c0:c0 + T1]),
                            reads=[xb0[jj], xb1[jj]], writes=[B_xr[xs_]])
                        t1 = cnt["tf"] % 2
                        cnt["tf"] += 1
                        T.op("dve", lambda e, t1=t1, jj=jj: e.tensor_tensor(out=tmpf[t1][:, :], in0=xy[:, jj, :], in1=rstd[:, :], op=ALU.mult),
                             reads=[B_xy[jj], B_rstd], writes=[B_tmpf[t1]])
                        T.op(PENG, lambda e, t1=t1, xs_=xs_: e.tensor_tensor(out=xo[xs_][:, :], in0=tmpf[t1][:, :], in1=xr[xs_][:, :], op=ALU.add),
                             reads=[B_tmpf[t1], B_xr[xs_]], writes=[B_xo[xs_]])
                        T.dma("pool", "fxo%d" % xs_, lambda e, xs_=xs_, jj=jj, c0=c0: e.dma_start(
                            out=Xdst[jj * 128:(jj + 1) * 128, c0:c0 + T1], in_=xo[xs_][:, :]),
                            reads=[B_xo[xs_]], writes=[xb0[jj], xb1[jj]])
                T.emit()

        ends = list(T.chlast.values())
        with nc.Block() as block:
            def fin(eng):
                for d in ends:
                    eng.wait_ge(d.sem, d.count)
            block.sync(fin)
    return nc


def rope_tables(pos):
    pos = np.asarray(pos, np.float32)
    S = pos.shape[0]
    out = np.zeros((4, 128, S), np.float32)
    inv = (np.float32(THETA) ** (-np.arange(0, HD, 2, dtype=np.float32) / np.float32(HD))).astype(np.float32)
    ang = pos[None, :] * inv[:, None]
    c, s = np.cos(ang), np.sin(ang)
    out[0, :64], out[0, 64:] = c, c
    out[1, :64], out[1, 64:] = -s, s
    invh = (np.float32(THETA) ** (-np.arange(0, HD // 2, 2, dtype=np.float32) / np.float32(HD // 2))).astype(np.float32)
    row = np.floor(pos / GRID_W).astype(np.float32)
    col = (pos - row * GRID_W).astype(np.float32)
    for base, p in ((0, row), (64, col)):
        a = p[None, :] * invh[:, None]
        c, s = np.cos(a), np.sin(a)
        out[2, base:base + 32], out[2, base + 32:base + 64] = c, c
        out[3, base:base + 32], out[3, base + 32:base + 64] = -s, s
    return out


def fm(v):
    return np.ascontiguousarray(np.asarray(v, np.float32).reshape(-1, 128).T)


def make_in_maps(inp, cfg=CFG):
    NS, DEPTH = cfg["NS"], cfg["DEPTH"]
    f32 = lambda a: np.ascontiguousarray(np.asarray(a, np.float32))
    xs, xp = f32(inp["x_sample"]), f32(inp["x_prompt"])
    cs, cp = f32(inp["c_sample"]), f32(inp["c_prompt"])
    assert xp.shape[1] == 2 * NS and xs.shape[1] == NS and xs.shape[0] == 8
    gT = np.concatenate([fm(f32(inp[k])[l]) for k in ("g_pre_mix", "g_post_mix", "g_pre_ffn", "g_post_ffn")
                         for l in range(DEPTH)], axis=1)
    bmodT = np.concatenate([fm(f32(inp["b_mod"])[l]) for l in range(DEPTH)], axis=1)
    qkg = np.stack([f32(inp[k])[l] for l in range(DEPTH) for k in ("q_norm_b", "k_norm_b")], axis=1)
    sinkb = np.ascontiguousarray(np.broadcast_to(f32(inp["sink_a"])[:DEPTH].reshape(1, -1), (128, DEPTH * 8)))
    kl = np.arange(128)[:, None]
    ql = np.arange(128)[None, :]
    tri_prev = (kl >= ql).astype(np.float32)
    tri_next = (kl <= ql).astype(np.float32)
    pm = np.zeros((128, 256), np.float32)
    for m_ in range(128):
        pm[(m_ + 64) % 128, m_] = 1.0
        pm[m_ ^ 32, 128 + m_] = 1.0
    shared = dict(gT=f32(gT), bmodT=f32(bmodT), qkg=f32(qkg), sinkb=sinkb, perms=pm)
    wnames = {"w_in": "w_in", "w_branch_a": "w_a", "w_branch_b": "w_b", "w_out": "w_out", "w_13": "w_13", "w_2": "w_2"}
    for k, dk in wnames.items():
        w = f32(inp[k])[:DEPTH]
        shared[dk] = w.reshape(-1, w.shape[-1])
    shared["w_mod"] = f32(inp["w_mod"])[:DEPTH]
    rope_s = rope_tables(np.arange(NS))
    rope_p = rope_tables(np.arange(2 * NS))
    maps = []
    for r in range(NCORES):
        m = dict(shared)
        if r == 0:
            xx, c0_, c1_ = xp[0], cp[0], cp[0]
            rp, valid = rope_p, 1.0
        else:
            i = min(r, 4) - 1
            xx = np.concatenate([xs[2 * i], xs[2 * i + 1]], axis=0)
            c0_, c1_ = cs[2 * i], cs[2 * i + 1]
            rp, valid = np.concatenate([rope_s, rope_s], axis=2), 0.0
        m["xT"] = np.ascontiguousarray(xx.T)
        m["c2T"] = np.ascontiguousarray(np.stack([fm(c0_), fm(c1_)], axis=2).reshape(128, KC * 2))
        m["ropeT"] = np.ascontiguousarray(rp)
        mk = [tri_prev, tri_next, tri_prev * valid, tri_next * valid]
        m["masks"] = np.ascontiguousarray(np.concatenate([np.tile(a, (1, 4)) for a in mk], axis=1))
        bm = np.zeros((128, 4), np.float32)
        if r != 0:
            bm[:, 1] = -30000.0
            bm[:, 2] = -30000.0
        m["biasm"] = bm
        maps.append(m)
    return maps


_NC_CACHE = {}


def kernel(**inputs):
    cfg = CFG
    key = tuple(sorted(cfg.items()))
    if key not in _NC_CACHE:
        _NC_CACHE[key] = build(cfg)
    nc = _NC_CACHE[key]
    in_maps = make_in_maps(inputs, cfg)
    res = run_bass_kernel_spmd(nc, in_maps, core_ids=list(range(NCORES)))
    return assemble(res, cfg)


def assemble(res, cfg=CFG):
    NS = cfg["NS"]
    ys = np.empty((8, NS, D), np.float32)
    yp = np.empty((1, 2 * NS, D), np.float32)
    yp[0] = np.asarray(res.results[0]["yT"], np.float32).T
    for i in range(4):
        y = np.asarray(res.results[1 + i]["yT"], np.float32).T
        ys[2 * i] = y[:NS]
        ys[2 * i + 1] = y[NS:]
    return (yp, ys)
```

```python
import contextlib
import numpy as np
import concourse.bass as bass
import concourse.mybir as mybir
from concourse.bass_utils import run_bass_kernel_spmd

F32 = mybir.dt.float32
BF16 = mybir.dt.bfloat16
ALU = mybir.AluOpType
AF = mybir.ActivationFunctionType

NCORES = 8
D = 2048
KC = D // 128
HD = 128
INW = 7168
DFF = 5632
FC = DFF // 128
GRID_W = 64
THETA = 10000.0
EPS = 1e-6
SCL = float(HD) ** -0.5

CFG = dict(NS=4096, NP=4096, DEPTH=4)


class Buf:
    __slots__ = ("w", "rs", "x")

    def __init__(self, x=False):
        self.w = None
        self.rs = {}
        self.x = x


def bufs(n):
    return [Buf() for _ in range(n)]


class Op:
    __slots__ = ("eng", "fn", "deps", "dma", "sem", "count", "need", "phase", "inc")


DBG = None


class Tracker:
    ENG = ("pe", "act", "dve", "pool", "sp")

    def __init__(self, nc, es):
        self.nc = nc
        self.es = es
        self.sem = {e: es.enter_context(nc.semaphore("s_" + e)) for e in ("pe", "act", "dve", "pool")}
        self.cnt = {e: 0 for e in self.sem}
        self.chan = {}
        self.chcnt = {}
        self.chlast = {}
        self.ops = {e: [] for e in self.ENG}
        self.waited = {e: {} for e in self.ENG}
        self.phase = 0
        self.last = {e: None for e in self.ENG}
        self.pending = {e: [] for e in self.ENG}
        self.nops = 0

    def _chan(self, name):
        if name not in self.chan:
            self.chan[name] = self.es.enter_context(self.nc.semaphore("c_" + name))
            self.chcnt[name] = 0
        return self.chan[name]

    def _mk(self, eng, fn, reads, writes):
        op = Op()
        op.eng = eng
        op.fn = fn
        op.dma = False
        op.need = False
        op.phase = self.phase
        op.count = None
        op.sem = None
        op.inc = 1
        if any(b.x for b in reads):
            writes = list(writes) + [b for b in reads if b.x]
            reads = [b for b in reads if not b.x]
        op_rw = (reads, writes)
        deps = []
        for b in reads:
            if b.w is not None:
                deps.append(b.w)
        for b in writes:
            if b.w is not None:
                deps.append(b.w)
            deps.extend(b.rs.values())
        out = []
        seen = set()
        if self.pending[eng]:
            for d in self.pending[eng]:
                seen.add(id(d))
                out.append(d)
            self.pending[eng] = []
        for d in deps:
            if d is op or id(d) in seen:
                continue
            seen.add(id(d))
            if not d.dma:
                if d.phase != op.phase:
                    continue
                if d.eng == eng and eng == "pe":
                    continue
                d.need = True
            out.append(d)
        op.deps = out
        self._rw = op_rw
        self.ops[eng].append(op)
        self.nops += 1
        return op

    def _reg(self, op, reads, writes):
        key = ("c", id(op.sem)) if op.dma else op.eng
        for b in writes:
            b.w = op
            b.rs = {}
        for b in reads:
            b.rs[key] = op

    def op(self, eng, fn, reads=(), writes=()):
        op = self._mk(eng, fn, reads, writes)
        reads, writes = self._rw
        self._reg(op, reads, writes)
        self.last[eng] = op
        return op

    def dma(self, eng, chan, fn, reads=(), writes=(), inc=16):
        sem = self._chan(chan)
        op = self._mk(eng, fn, reads, writes)
        op.dma = True
        op.sem = sem
        op.inc = inc
        self.chcnt[chan] += inc
        op.count = self.chcnt[chan]
        self.chlast[chan] = op
        reads, writes = self._rw
        self._reg(op, reads, writes)
        return op

    def emit(self):
        nc = self.nc
        for e in ("pe", "act", "dve", "pool"):
            for op in self.ops[e]:
                if not op.dma and op.need:
                    self.cnt[e] += 1
                    op.count = self.cnt[e]
                    op.sem = self.sem[e]
        ends = self.barrier_ops()
        hw = {"pe": "tensor", "act": "scalar", "dve": "vector", "pool": "gpsimd", "sp": "sync"}
        with nc.Block() as block:
            for e in self.ENG:
                ops = self.ops[e]
                waited = self.waited[e]

                def body(eng, ops=ops, waited=waited):
                    for op in ops:
                        wv = {}
                        for d in op.deps:
                            key = id(d.sem)
                            if key not in wv or wv[key][1] < d.count:
                                wv[key] = (d.sem, d.count)
                        for key, (sm, c) in wv.items():
                            if waited.get(key, 0) < c:
                                eng.wait_ge(sm, c)
                                waited[key] = c
                                if DBG is not None:
                                    DBG.write("%s WAIT %s >= %d\n" % (op.eng, getattr(sm, "name", sm), c))
                        ins = op.fn(eng)
                        if DBG is not None:
                            DBG.write("%s OP %s%s\n" % (op.eng, ins.concise()[:150].replace("\n", " "), (" INC %s -> %s" % (getattr(op.sem, "name", op.sem), op.count)) if (op.dma or op.need) else ""))
                        if op.dma:
                            ins.then_inc(op.sem, op.inc)
                        elif op.need:
                            ins.then_inc(op.sem, 1)

                getattr(block, hw[e])(body)
        self.ops = {e: [] for e in self.ENG}
        self.phase += 1
        for e in self.ENG:
            self.pending[e] = list(ends)
        if (self.phase - 1) % 3 == 0 and self.phase < 12:
            self.sem = {e: self.es.enter_context(self.nc.semaphore("s%d_%s" % (self.phase, e))) for e in ("pe", "act", "dve", "pool")}
            self.cnt = {e: 0 for e in self.sem}

    def barrier_ops(self):
        ends = []
        for e in ("pe", "act", "dve", "pool"):
            lo = None
            for op in reversed(self.ops[e]):
                if not op.dma:
                    lo = op
                    break
            if lo is not None:
                if lo.count is None:
                    self.cnt[e] += 1
                    lo.count = self.cnt[e]
                    lo.sem = self.sem[e]
                    lo.need = True
                ends.append(lo)
        ends.extend(self.chlast.values())
        return ends


def build(cfg=CFG):
    NS, NP, DEPTH = cfg["NS"], cfg["NP"], cfg["DEPTH"]
    NT = NS + NP
    T1 = 512
    T2 = 256
    NT1 = NT // T1
    NST1 = NS // T1
    NPT1 = NP // T1
    assert NP == NS
    NB_S = NS // 128
    NB = NT // 128
    NKB_MAX = NB
    NEXT = NB + 2

    nc = bass.Bass("TRN2", target_bir_lowering=False)
    PENG = ("pool", "dve")[cfg.get("peng", 0)]
    STQ = ("pool", "sp")[cfg.get("stq", 0)]

    def din(name, shape, dt=F32):
        return nc.dram_tensor(name, shape, dt, kind="ExternalInput")

    xT = din("xT", [D, NT])
    gT = din("gT", [128, 4 * DEPTH * KC])
    qkg = din("qkg", [128, DEPTH * 2])
    sinkb = din("sinkb", [128, DEPTH * 8])
    ropeT = din("ropeT", [4, 128, NT])
    masks = din("masks", [128, 4 * 512])
    perms = din("perms", [128, 256])
    WSH = {"in": (D, INW), "a": (1024, D), "b": (1024, D), "out": (D, D), "13": (D, 2 * DFF), "2": (DFF, D)}
    w_sh = {k: din("w_" + k, [DEPTH * r, c]) for k, (r, c) in WSH.items()}
    w_mod = din("w_mod", [DEPTH, D, 6 * D])
    bmodT = din("bmodT", [128, DEPTH * 96])
    c2T = din("c2T", [128, KC * 2])
    biasm = din("biasm", [128, 4])
    yT = nc.dram_tensor("yT", [D, NT], F32, kind="ExternalOutput")

    w_bf = {k: nc.dram_tensor("wb_" + k, [DEPTH * r, c], BF16) for k, (r, c) in WSH.items()}
    wb_in, wb_a, wb_b, wb_out, wb_13, wb_2 = [
        [w_bf[k][l * WSH[k][0]:(l + 1) * WSH[k][0], :] for l in range(DEPTH)] for k in ("in", "a", "b", "out", "13", "2")]
    XS = nc.dram_tensor("XS", [D, NT], F32)
    QA = nc.dram_tensor("QA", [128, 8, NT], BF16)
    QB = nc.dram_tensor("QB", [128, 8, NT], BF16)
    GT = nc.dram_tensor("GT", [128, 32, NT], BF16)
    KAx = nc.dram_tensor("KAx", [128, 2, NEXT * 128], BF16)
    VAx = nc.dram_tensor("VAx", [NEXT, 128, 256], BF16)
    KBd = nc.dram_tensor("KBd", [128, 2, NT], BF16)
    VBd = nc.dram_tensor("VBd", [NB, 128, 256], BF16)

    es = contextlib.ExitStack()
    with es:
        T = Tracker(nc, es)

        uniq = [0]

        def sb(name, shape, dt, st=es):
            uniq[0] += 1
            return st.enter_context(nc.sbuf_tensor("%s_u%d" % (name, uniq[0]), shape, dt))

        banks = [es.enter_context(nc.psum_tensor("bank%d" % i, [128, 512], F32)) for i in range(8)]
        bankB = [Buf(x=True) for _ in range(8)]

        ones_d = sb("ones_d", [128, 128], BF16)
        ones_h = sb("ones_h", [128, 128], BF16)
        ones_1 = sb("ones_1", [128, 128], BF16)
        perm64 = sb("perm64", [128, 128], BF16)
        perm32 = sb("perm32", [128, 128], BF16)
        maskb = sb("maskb", [128, 4, 512], BF16)
        bsm = sb("bsm", [128, 4], F32)
        modT = sb("modT", [128, DEPTH, 96, 2], F32)
        AmT = sb("AmT", [128, DEPTH, 2, KC], F32)
        GGm = sb("GGm", [128, DEPTH, 2, KC], F32)
        AfT = sb("AfT", [128, DEPTH, 2, KC], F32)
        GGf = sb("GGf", [128, DEPTH, 2, KC], F32)
        qkgs = sb("qkgs", [128, DEPTH * 2], F32)
        sinke = sb("sinke", [128, DEPTH * 8], F32)
        epsb = sb("epsb", [128, 1], F32)
        B_const = Buf()

        B_xs = [bufs(KC) for _ in range(NT // T2)]
        B_q = bufs(NT1)
        B_g = bufs(NT1)
        B_kax = bufs(NEXT)
        B_vax = bufs(NEXT)
        B_kb = bufs(NT1)
        B_vb = bufs(NT1)
        B_w = {}

        with contextlib.ExitStack() as ps:
            cst = sb("cst", [128, KC * 2], F32, ps)
            gts = sb("gts", [128, 4 * DEPTH * KC], F32, ps)
            bms = sb("bms", [128, DEPTH * 96], F32, ps)
            mks = sb("mks", [128, 4 * 512], F32, ps)
            pms = sb("pms", [128, 256], F32, ps)
            scf = sb("scf", [128, KC * 2], F32, ps)
            B_ld = Buf()
            for dst_, src_ in ((cst, c2T), (gts, gT), (bms, bmodT), (mks, masks), (pms, perms), (bsm, biasm),
                               (qkgs, qkg), (sinke, sinkb)):
                T.dma("sp", "cld", lambda e, d=dst_, s=src_: e.dma_start(out=d[:, :], in_=s[:, :]), writes=[B_ld])

            ci = 0
            for l in range(DEPTH):
                for key in ("in", "a", "b", "out", "13", "2"):
                    r_, c_ = WSH[key]
                    n = r_ * c_ // 2048
                    s2 = w_sh[key][l * r_:(l + 1) * r_, :].rearrange("r c -> (r c)").rearrange("(n f) -> n f", f=2048)
                    d2 = w_bf[key][l * r_:(l + 1) * r_, :].rearrange("r c -> (r c)").rearrange("(n f) -> n f", f=2048)
                    bl = Buf()
                    B_w[(key, l)] = bl
                    for i0 in range(0, n, 1024):
                        i1 = min(n, i0 + 1024)
                        T.dma("pool", "cast%d" % (ci % 4),
                              lambda e, d2=d2, s2=s2, i0=i0, i1=i1: e.dma_start(out=d2[i0:i1, :], in_=s2[i0:i1, :]),
                              writes=[bl])
                        ci += 1

            def memset(t, ap, v):
                T.op("dve", lambda e, ap=ap, v=v: e.memset(ap, v), writes=[B_const])

            memset(None, ones_d[:, :], 1.0 / D)
            memset(None, ones_h[:, :], 1.0 / HD)
            memset(None, ones_1[:, :], 1.0)
            memset(None, epsb[:, :], EPS)
            T.op("dve", lambda e: e.tensor_copy(out=perm64[:, :], in_=pms[:, 0:128]), reads=[B_ld], writes=[B_const])
            T.op("dve", lambda e: e.tensor_copy(out=perm32[:, :], in_=pms[:, 128:256]), reads=[B_ld], writes=[B_const])
            T.op("dve", lambda e: e.tensor_copy(out=maskb[:, :, :].rearrange("p a b -> p (a b)"), in_=mks[:, :]),
                 reads=[B_ld], writes=[B_const])
            zt = sb("zt", [128, 256], BF16, ps)
            B_zt = Buf()
            T.op("dve", lambda e: e.memset(zt[:, :], 0.0), writes=[B_zt])
            for eb in (0, NB + 1):
                T.dma("sp", "zst", lambda e, eb=eb: e.dma_start(
                    out=KAx[:, :, eb * 128:(eb + 1) * 128], in_=zt[:, :].rearrange("p (h t) -> p h t", h=2)),
                    reads=[B_zt], writes=[B_kax[eb]])
                T.dma("sp", "zst", lambda e, eb=eb: e.dma_start(out=VAx[eb], in_=zt[:, :]), reads=[B_zt], writes=[B_vax[eb]])
            T.op("act", lambda e: e.activation(out=sinke[:, :], in_=sinke[:, :], func=AF.Exp), reads=[B_ld], writes=[B_ld])
            T.op("act", lambda e: e.activation(out=scf[:, :], in_=cst[:, :], func=AF.Silu), reads=[B_ld], writes=[B_ld])
            scf3 = scf[:, :].rearrange("p (k s) -> p k s", s=2)

            MB = 768
            wm = [sb("wm%d" % i, [128, KC, MB], F32, ps) for i in range(2)]
            B_wm = bufs(2)
            nblk = 6 * D // MB
            it = 0
            for l in range(DEPTH):
                for bi in range(nblk):
                    s_ = it % 2
                    src = w_mod[l].rearrange("(k p) n -> p k n", p=128)
                    for kk in range(0, KC, 4):
                        T.dma("sp", "wm%d" % s_,
                              lambda e, s_=s_, src=src, bi=bi, kk=kk: e.dma_start(
                                  out=wm[s_][:, kk:kk + 4, :], in_=src[:, kk:kk + 4, bi * MB:(bi + 1) * MB]),
                              writes=[B_wm[s_]])
                    bk = it % 2
                    for jj in range(MB // 128):
                        for k in range(KC):
                            T.op("pe", lambda e, s_=s_, jj=jj, k=k, bk=bk: e.matmul(
                                banks[bk][:, jj * 2:jj * 2 + 2], lhsT=wm[s_][:, k, jj * 128:(jj + 1) * 128],
                                rhs=scf3[:, k, :], start=(k == 0), stop=(k == KC - 1)),
                                reads=[B_wm[s_], B_ld], writes=[bankB[bk]])
                    j0 = bi * (MB // 128)
                    for sg in range(2):
                        T.op("dve", lambda e, l=l, j0=j0, sg=sg, bk=bk: e.tensor_tensor(
                            out=modT[:, l, j0:j0 + MB // 128, sg],
                            in0=banks[bk][:, 0:2 * (MB // 128)].rearrange("p (j s) -> p j s", s=2)[:, :, sg],
                            in1=bms[:, l * 96 + j0:l * 96 + j0 + MB // 128], op=ALU.add),
                            reads=[bankB[bk], B_ld], writes=[B_const])
                    it += 1
            gts4 = gts[:, :].rearrange("p (a l k) -> p a l k", a=4, l=DEPTH)
            for l in range(DEPTH):
                for sg in range(2):
                    def mk(dst, gk, sc_j, mode, l=l, sg=sg):
                        if mode == "A":
                            T.op("dve", lambda e: e.scalar_tensor_tensor(
                                out=dst[:, l, sg, :], in0=modT[:, l, sc_j:sc_j + KC, sg], scalar=1.0,
                                in1=gts4[:, gk, l, :], op0=ALU.add, op1=ALU.mult),
                                reads=[B_const, B_ld], writes=[B_const])
                        else:
                            T.op("dve", lambda e: e.tensor_tensor(
                                out=dst[:, l, sg, :], in0=modT[:, l, sc_j:sc_j + KC, sg],
                                in1=gts4[:, gk, l, :], op=ALU.mult),
                                reads=[B_const, B_ld], writes=[B_const])
                    mk(AmT, 0, 16, "A")
                    mk(GGm, 1, 32, "G")
                    mk(AfT, 2, 64, "A")
                    mk(GGf, 3, 80, "G")
            T.emit()

        def seg_of_tok(c0):
            return 0 if c0 < NS else 1

        def rsqrt_op(out_ap, in_ap, rd, wb):
            T.op("act", lambda e: e.activation(out=out_ap, in_=in_ap, func=AF.Sqrt, bias=epsb[:, 0:1]),
                 reads=rd + [B_const], writes=[wb])
            T.op("dve", lambda e: e.reciprocal(out=out_ap, in_=out_ap), reads=[wb], writes=[wb])

        def norm_mod(xt, B_x, uT, B_u, sq, B_sq, rstd, B_rstd, tmpf, B_tmpf, ssbank, l, sg, A, Bj0, W):
            for k in range(KC):
                s = k % len(sq)
                T.op("act", lambda e, k=k, s=s: e.activation(out=sq[s][:, :W], in_=xt[:, k, :W], func=AF.Square),
                     reads=[B_x[k]], writes=[B_sq[s]])
                T.op("pe", lambda e, k=k, s=s: e.matmul(banks[ssbank][:, :W], lhsT=ones_d[:, :], rhs=sq[s][:, :W],
                                                        start=(k == 0), stop=(k == KC - 1)),
                     reads=[B_sq[s], B_const], writes=[bankB[ssbank]])
            rsqrt_op(rstd[:, :W], banks[ssbank][:, :W], [bankB[ssbank]], B_rstd)
            for k in range(KC):
                s = k % len(tmpf)
                T.op("dve", lambda e, k=k, s=s: e.tensor_tensor(out=tmpf[s][:, :W], in0=xt[:, k, :W], in1=rstd[:, :W],
                                                                op=ALU.mult),
                     reads=[B_x[k], B_rstd], writes=[B_tmpf[s]])
                T.op("act", lambda e, k=k, s=s: e.activation(
                    out=uT[:, k, :W], in_=tmpf[s][:, :W], func=AF.Identity,
                    scale=A[:, l, sg, k:k + 1], bias=modT[:, l, Bj0 + k, sg:sg + 1]),
                    reads=[B_tmpf[s], B_const], writes=[B_u[k]])

        STOP = cfg.get("stop", 99)
        for l in range(DEPTH):
            if STOP == 0:
                break
            Xsrc = xT if l == 0 else XS
            Xdst = yT if l == DEPTH - 1 else XS
            par = l % 2
            with contextlib.ExitStack() as ps:
                xt = sb("p1_x", [128, KC, T1], F32, ps)
                uT = [sb("p1_u%d" % i, [128, KC, T1], BF16, ps) for i in range(2)]
                wr = [sb("p1_w%d" % i, [128, KC, 512], BF16, ps) for i in range(4)]
                stg = [sb("p1_s%d" % i, [128, 4, T1], BF16, ps) for i in range(3)]
                vst = [sb("p1_v%d" % i, [128, 4, 256], BF16, ps) for i in range(2)]
                rope = [sb("p1_r%d" % i, [128, 4, T1], F32, ps) for i in range(2)]
                sq = [sb("p1_sq%d" % i, [128, T1], BF16, ps) for i in range(3)]
                tmpf = [sb("p1_t%d" % i, [128, T1], F32, ps) for i in range(4)]
                rstd = sb("p1_rstd", [128, T1], F32, ps)
                rsth = [sb("p1_rh%d" % i, [128, T1], F32, ps) for i in range(2)]
                qsb = [sb("p1_qs%d" % i, [128, T1], BF16, ps) for i in range(2)]
                qn = [sb("p1_qn%d" % i, [128, T1], F32, ps) for i in range(2)]
                B_x = bufs(KC)
                B_u = [bufs(KC) for _ in range(2)]
                B_wr = bufs(4)
                B_stg = bufs(3)
                B_vst = bufs(2)
                B_rope = bufs(2)
                B_sq = bufs(3)
                B_tmpf = bufs(4)
                B_rstd = Buf()
                B_rsth = bufs(2)
                B_qsb = bufs(2)
                B_qn = bufs(2)
                cnt = dict(w=0, stg=0, vst=0, z=0, aux=0, t=0, q=0)
                Wl = wb_in[l].rearrange("(k p) n -> p k n", p=128)

                order = list(range(NT1))

                def prefetch(ti):
                    t = order[ti]
                    c0 = t * T1
                    sg = 0 if t < NST1 else 1
                    ub = ti % 2
                    rb = ti % 2
                    T.dma("sp", "p1x", lambda e, c0=c0: e.dma_start(
                        out=xt[:, :, :], in_=Xsrc.ap().rearrange("(k p) t -> p k t", p=128)[:, :, c0:c0 + T1]),
                        reads=B_xs[c0 // T2] + B_xs[c0 // T2 + 1], writes=B_x)
                    T.dma("sp", "p1r%d" % rb, lambda e, c0=c0, rb=rb: e.dma_start(
                        out=rope[rb][:, :, :], in_=ropeT.ap().rearrange("f p t -> p f t")[:, :, c0:c0 + T1]),
                        writes=[B_rope[rb]])
                    norm_mod(xt, B_x, uT[ub], B_u[ub], sq, B_sq, rstd, B_rstd, tmpf, B_tmpf, 4, l, sg, AmT, 0, T1)

                order = order[:cfg.get("p1_tiles", 99)]
                prefetch(0)
                for ti, t in enumerate(order):
                    c0 = t * T1
                    sg = 0 if t < NST1 else 1
                    tp = t - NST1
                    ub = ti % 2
                    rb = ti % 2

                    for blk in range(cfg.get("p1_blks", 14)):
                        if blk == 9 and ti + 1 < len(order):
                            prefetch(ti + 1)
                        ws = cnt["w"] % 4
                        cnt["w"] += 1
                        T.dma("sp", "p1w%d" % ws, lambda e, ws=ws, blk=blk: e.dma_start(
                            out=wr[ws][:, :, :], in_=Wl[:, :, blk * 512:(blk + 1) * 512]),
                            reads=[B_w[("in", l)]], writes=[B_wr[ws]])
                        kind = ("qa", "qa", "kva", "qb", "qb", "kvb", "g", "g", "g", "g", "g", "g", "g", "g")[blk]
                        nfm = 2 if kind in ("kva", "kvb") else 4
                        ss_ = cnt["stg"] % 3
                        cnt["stg"] += 1
                        for j in range(nfm):
                            zb = cnt["z"] % 4
                            cnt["z"] += 1
                            for rep in range(cfg.get("zrep", 1)):
                              for k in range(KC):
                                T.op("pe", lambda e, ws=ws, j=j, k=k, zb=zb, ub=ub, rep=rep: e.matmul(
                                    banks[zb][:, :], lhsT=wr[ws][:, k, j * 128:(j + 1) * 128], rhs=uT[ub][:, k, :],
                                    start=(k == 0 and rep == 0), stop=(k == KC - 1)),
                                    reads=[B_wr[ws], B_u[ub][k]], writes=[bankB[zb]])
                            if cfg.get("p1_epi", 2) == 0:
                                continue
                            if kind == "g":
                                T.op("act", lambda e, zb=zb, ss_=ss_, j=j: e.activation(
                                    out=stg[ss_][:, j, :], in_=banks[zb][:, :], func=AF.Sigmoid),
                                    reads=[bankB[zb]], writes=[B_stg[ss_]])
                                continue
                            isb = kind in ("qb", "kvb")
                            a = cnt["aux"] % 2
                            cnt["aux"] += 1
                            ab = cfg.get("abase", 5) + a
                            if isb:
                                gcol = l * 2 + (0 if kind == "qb" else 1)
                                q_ = cnt["q"] % 3
                                cnt["q"] += 1
                                T.op("act", lambda e, zb=zb, q_=q_: e.activation(out=sq[q_][:, :], in_=banks[zb][:, :], func=AF.Square),
                                     reads=[bankB[zb]], writes=[B_sq[q_]])
                                T.op("pe", lambda e, q_=q_, ab=ab: e.matmul(banks[ab][:, :], lhsT=ones_h[:, :], rhs=sq[q_][:, :],
                                                                            start=True, stop=True),
                                     reads=[B_sq[q_], B_const], writes=[bankB[ab]])
                                rsqrt_op(rsth[a][:, :], banks[ab][:, :], [bankB[ab]], B_rsth[a])
                                T.op("dve", lambda e, a=a, zb=zb, gcol=gcol: e.scalar_tensor_tensor(
                                    out=qn[a][:, :], in0=banks[zb][:, :], scalar=qkgs[:, gcol:gcol + 1], in1=rsth[a][:, :],
                                    op0=ALU.mult, op1=ALU.mult), reads=[bankB[zb], B_rsth[a], B_ld], writes=[B_qn[a]])
                                T.op("act", lambda e, a=a: e.activation(out=qsb[a][:, :], in_=qn[a][:, :], func=AF.Identity),
                                     reads=[B_qn[a]], writes=[B_qsb[a]])
                                src_ap = qn[a][:, :]
                                srcB = B_qn[a]
                                pm = perm32
                                ci, si = 2, 3
                            else:
                                T.op("act", lambda e, a=a, zb=zb: e.activation(out=qsb[a][:, :], in_=banks[zb][:, :], func=AF.Identity),
                                     reads=[bankB[zb]], writes=[B_qsb[a]])
                                src_ap = banks[zb][:, :]
                                srcB = bankB[zb]
                                pm = perm64
                                ci, si = 0, 1
                            EL = cfg.get("epi_lvl", 9)
                            if EL < 2:
                                continue
                            if cfg.get("dbg_pm", 0) == 1:
                                pm = ones_h
                            if cfg.get("dbg_pm", 0) == 2:
                                T.op("pe", lambda e, a=a, ab=ab, pm=pm, ub=ub: e.matmul(banks[ab][:, :], lhsT=pm[:, :], rhs=uT[ub][:, 0, :],
                                                                                 start=True, stop=True),
                                     reads=[B_u[ub][0], B_const], writes=[bankB[ab]])
                                continue
                            T.op("pe", lambda e, a=a, ab=ab, pm=pm: e.matmul(banks[ab][:, :], lhsT=pm[:, :], rhs=qsb[a][:, :],
                                                                             start=True, stop=True),
                                 reads=[B_qsb[a], B_const], writes=[bankB[ab]])
                            if EL < 3:
                                continue
                            t1 = cnt["t"] % 4
                            t2 = (cnt["t"] + 1) % 4
                            cnt["t"] += 2
                            T.op("dve", lambda e, t1=t1, src_ap=src_ap, rb=rb, ci=ci: e.tensor_tensor(
                                out=tmpf[t1][:, :], in0=src_ap, in1=rope[rb][:, ci, :], op=ALU.mult),
                                reads=[srcB, B_rope[rb]], writes=[B_tmpf[t1]])
                            T.op("dve", lambda e, t2=t2, ab=ab, rb=rb, si=si: e.tensor_tensor(
                                out=tmpf[t2][:, :], in0=banks[ab][:, :], in1=rope[rb][:, si, :], op=ALU.mult),
                                reads=[bankB[ab], B_rope[rb]], writes=[B_tmpf[t2]])
                            if EL < 4:
                                continue
                            T.op(PENG, lambda e, t1=t1, t2=t2, ss_=ss_, j=j: e.tensor_tensor(
                                out=stg[ss_][:, j, :], in0=tmpf[t1][:, :], in1=tmpf[t2][:, :], op=ALU.add),
                                reads=[B_tmpf[t1], B_tmpf[t2]], writes=[B_stg[ss_]])
                        if kind in ("kva", "kvb"):
                            vs = cnt["vst"] % 2
                            cnt["vst"] += 1
                            for s4 in range(4):
                                zb = cnt["z"] % 4
                                cnt["z"] += 1
                                for k in range(KC):
                                    T.op("pe", lambda e, ws=ws, s4=s4, k=k, zb=zb, ub=ub: e.matmul(
                                        banks[zb][:, 0:256], lhsT=uT[ub][:, k, s4 * 128:(s4 + 1) * 128], rhs=wr[ws][:, k, 256:512],
                                        start=(k == 0), stop=(k == KC - 1)),
                                        reads=[B_wr[ws], B_u[ub][k]], writes=[bankB[zb]])
                                T.op("dve", lambda e, zb=zb, vs=vs, s4=s4: e.tensor_copy(out=vst[vs][:, s4, :], in_=banks[zb][:, 0:256]),
                                     reads=[bankB[zb]], writes=[B_vst[vs]])
                        if cfg.get("p1_epi", 2) < 2:
                            continue
                        def st(chan, fn, reads, writes):
                            T.dma(STQ, chan, fn, reads=reads, writes=writes)
                        sch = "p1s%d" % ss_
                        if kind == "qa" or kind == "qb":
                            dst = QA if kind == "qa" else QB
                            h0 = 0 if blk in (0, 3) else 4
                            st(sch, lambda e, dst=dst, h0=h0, ss_=ss_, c0=c0: e.dma_start(
                                out=dst[:, h0:h0 + 4, c0:c0 + T1], in_=stg[ss_][:, :, :]),
                                [B_stg[ss_]], [B_q[t]])
                        elif kind == "g":
                            j0 = (blk - 6) * 4
                            st(sch, lambda e, j0=j0, ss_=ss_, c0=c0: e.dma_start(
                                out=GT[:, j0:j0 + 4, c0:c0 + T1], in_=stg[ss_][:, :, :]),
                                [B_stg[ss_]], [B_g[t]])
                        elif kind == "kva":
                            eb = t * 4 + 1
                            st(sch, lambda e, ss_=ss_, eb=eb: e.dma_start(
                                out=KAx[:, :, eb * 128:(eb + 4) * 128], in_=stg[ss_][:, 0:2, :]),
                                [B_stg[ss_]], B_kax[eb:eb + 4])
                            st("p1v%d" % vs, lambda e, vs=vs, eb=eb: e.dma_start(
                                out=VAx[eb:eb + 4].rearrange("b p c -> p b c"), in_=vst[vs][:, :, :]),
                                [B_vst[vs]], B_vax[eb:eb + 4])
                        elif kind == "kvb":
                            st(sch, lambda e, ss_=ss_, c0=c0: e.dma_start(out=KBd[:, :, c0:c0 + T1], in_=stg[ss_][:, 0:2, :]),
                               [B_stg[ss_]], [B_kb[t]])
                            st("p1v%d" % vs, lambda e, vs=vs, t=t: e.dma_start(
                                out=VBd[t * 4:t * 4 + 4].rearrange("b p c -> p b c"), in_=vst[vs][:, :, :]),
                                [B_vst[vs]], [B_vb[t]])
                T.emit()
            if STOP == 1:
                break

            with contextlib.ExitStack() as ps:
                KBs = sb("a_kb", [128, 2, NKB_MAX * 128], BF16, ps)
                VBs = sb("a_vb", [128, NKB_MAX, 256], BF16, ps)
                qa_s = [sb("a_qa%d" % i, [128, 8, T2], BF16, ps) for i in range(2)]
                qb_s = [sb("a_qb%d" % i, [128, 8, T2], BF16, ps) for i in range(2)]
                gt_s = [sb("a_g%d" % i, [128, 8, T2], BF16, ps) for i in range(2)]
                kw = [sb("a_kw%d" % i, [128, 2, 512], BF16, ps) for i in range(2)]
                vw = [sb("a_vw%d" % i, [128, 4, 256], BF16, ps) for i in range(2)]
                oa = sb("a_oa", [128, 8, T2], BF16, ps)
                ob = sb("a_ob", [128, 8, T2], BF16, ps)
                mg = sb("a_mg", [128, KC, T2], BF16, ps)
                yg = sb("a_yg", [128, KC, T2], F32, ps)
                wr = [sb("a_w%d" % i, [128, KC, 512], BF16, ps) for i in range(2)]
                pb = [sb("a_p%d" % i, [128, 512], BF16, ps) for i in range(4)]
                dn = [sb("a_dn%d" % i, [128, 512], F32, ps) for i in range(2)]
                tf = [sb("a_tf%d" % i, [128, T2], F32, ps) for i in range(4)]
                sq = [sb("a_sq%d" % i, [128, T2], BF16, ps) for i in range(3)]
                xr = [sb("a_x%d" % i, [128, 2, T2], F32, ps) for i in range(2)]
                xo = [sb("a_xo%d" % i, [128, 2, T2], F32, ps) for i in range(2)]
                rstd = sb("a_rstd", [128, T2], F32, ps)
                sexp = sb("a_sexp", [128, 8, 128], F32, ps)
                onesf = sb("a_onesf", [128, 128], F32, ps)
                B_sexp = Buf()
                T.op("dve", lambda e: e.memset(onesf[:, :], 1.0), writes=[B_sexp])
                for hh in range(8):
                    T.op("dve", lambda e, hh=hh: e.tensor_scalar(
                        out=sexp[:, hh, :], in0=onesf[:, :], scalar1=sinke[:, l * 8 + hh:l * 8 + hh + 1], scalar2=None,
                        op0=ALU.mult), reads=[B_sexp], writes=[B_sexp])
                B_KB, B_VB = Buf(), Buf()
                B_qa, B_qb, B_gt = bufs(2), bufs(2), bufs(2)
                B_kw, B_vw = bufs(2), bufs(2)
                B_oa, B_ob = bufs(8), bufs(8)
                B_mg, B_yg = bufs(KC), bufs(KC)
                B_wr = bufs(2)
                B_pb, B_dn, B_tf, B_sq = bufs(4), bufs(2), bufs(4), bufs(3)
                B_xr, B_xo = bufs(2), bufs(2)
                B_rstd = Buf()
                cnt = dict(p=0, s=0, od=0, dn=0, w=0, z=0, tf=0, sq=0, x=0, g=0)
                Wa = wb_a[l].rearrange("(k p) n -> p k n", p=128)
                Wb = wb_b[l].rearrange("(k p) n -> p k n", p=128)
                Wo = wb_out[l].rearrange("(k p) n -> p k n", p=128)
                NT2 = NT // T2
                NST2 = NS // T2
                for tt in range(NT2):
                    c0 = tt * T2
                    sg = 0 if tt < NST2 else 1
                    if tt == 0:
                        for t in range(NT1):
                            T.dma("sp", "akb", lambda e, t=t: e.dma_start(out=KBs[:, :, t * T1:(t + 1) * T1], in_=KBd[:, :, t * T1:(t + 1) * T1]),
                                  reads=[B_kb[t]], writes=[B_KB])
                            T.dma("sp", "avb", lambda e, t=t: e.dma_start(
                                out=VBs[:, t * 4:t * 4 + 4, :], in_=VBd[t * 4:t * 4 + 4].rearrange("b p c -> p b c")),
                                reads=[B_vb[t]], writes=[B_VB])
                    qs = tt % 2
                    t1i = c0 // T1
                    T.dma("sp", "aqa%d" % qs, lambda e, qs=qs, c0=c0: e.dma_start(out=qa_s[qs][:, :, :], in_=QA[:, :, c0:c0 + T2]),
                          reads=[B_q[t1i]], writes=[B_qa[qs]])
                    T.dma("sp", "aqb%d" % qs, lambda e, qs=qs, c0=c0: e.dma_start(out=qb_s[qs][:, :, :], in_=QB[:, :, c0:c0 + T2]),
                          reads=[B_q[t1i]], writes=[B_qb[qs]])
                    b0 = c0 // 128
                    e0 = b0
                    T.dma("sp", "akw%d" % qs, lambda e, qs=qs, e0=e0: e.dma_start(out=kw[qs][:, :, :], in_=KAx[:, :, e0 * 128:(e0 + 4) * 128]),
                          reads=B_kax[e0:e0 + 4], writes=[B_kw[qs]])
                    T.dma("sp", "avw%d" % qs, lambda e, qs=qs, e0=e0: e.dma_start(
                        out=vw[qs][:, :, :], in_=VAx[e0:e0 + 4].rearrange("b p c -> p b c")),
                        reads=B_vax[e0:e0 + 4], writes=[B_vw[qs]])
                    nkb = NB

                    def attn(qsrc, B_qsrc, osb, B_o, window):
                        for h in range(2):
                            for s in range(2):
                                qap = qsrc[:, 4 * h:4 * h + 4, s * 128:(s + 1) * 128]
                                if window:
                                    kbl = []
                                    for d_ in range(3):
                                        lb = b0 + s + d_ - 1
                                        if lb < 0 or lb >= NB:
                                            continue
                                        cross = (lb // NB_S) != ((b0 + s) // NB_S)
                                        mk = None
                                        if d_ == 0:
                                            mk = 2 if cross else 0
                                        if d_ == 2:
                                            mk = 3 if cross else 1
                                        kbl.append((s + d_, mk, None))
                                else:
                                    kbl = [(i, None, 2 * sg + (i // NB_S)) for i in range(nkb)]
                                ob_ = 3 + cnt["od"] % 2
                                db_ = 5 + cnt["od"] % 2
                                cnt["od"] += 1
                                LA = 2
                                pend = []
                                nk = len(kbl)
                                for i in range(nk + LA):
                                    if i < nk:
                                        kb_, mk, bi_ = kbl[i]
                                        sbk = cnt["s"] % 3
                                        cnt["s"] += 1
                                        p_ = cnt["p"] % 4
                                        cnt["p"] += 1
                                        if window:
                                            lhsK = kw[qs][:, h, kb_ * 128:(kb_ + 1) * 128]
                                            lhsV = vw[qs][:, kb_, h * 128:(h + 1) * 128]
                                            rK, rV = [B_kw[qs]], [B_vw[qs]]
                                        else:
                                            lhsK = KBs[:, h, kb_ * 128:(kb_ + 1) * 128]
                                            lhsV = VBs[:, kb_, h * 128:(h + 1) * 128]
                                            rK, rV = [B_KB], [B_VB]
                                        T.op("pe", lambda e, sbk=sbk, lhsK=lhsK, qap=qap: e.matmul(
                                            banks[sbk][:, :].rearrange("p (a b) -> p a b", a=4), lhsT=lhsK, rhs=qap, start=True, stop=True),
                                            reads=rK + [B_qsrc], writes=[bankB[sbk]])
                                        if bi_ is None:
                                            T.op("act", lambda e, sbk=sbk, p_=p_: e.activation(out=pb[p_][:, :], in_=banks[sbk][:, :], func=AF.Exp, scale=SCL),
                                                 reads=[bankB[sbk]], writes=[B_pb[p_]])
                                        else:
                                            T.op("act", lambda e, sbk=sbk, p_=p_, bi_=bi_: e.activation(
                                                out=pb[p_][:, :], in_=banks[sbk][:, :], func=AF.Exp, scale=SCL, bias=bsm[:, bi_:bi_ + 1]),
                                                reads=[bankB[sbk], B_const], writes=[B_pb[p_]])
                                        if mk is not None:
                                            T.op(PENG, lambda e, p_=p_, mk=mk: e.tensor_tensor(out=pb[p_][:, :], in0=pb[p_][:, :], in1=maskb[:, mk, :], op=ALU.mult),
                                                 reads=[B_pb[p_], B_const], writes=[B_pb[p_]])
                                        pend.append((p_, lhsV, rV, i == 0, i == nk - 1))
                                    if i >= LA:
                                        p_, lhsV, rV, first, last = pend[i - LA]
                                        T.op("pe", lambda e, ob_=ob_, lhsV=lhsV, p_=p_, first=first, last=last: e.matmul(
                                            banks[ob_][:, :], lhsT=lhsV, rhs=pb[p_][:, :], start=first, stop=last),
                                            reads=rV + [B_pb[p_]], writes=[bankB[ob_]])
                                        T.op("pe", lambda e, db_=db_, p_=p_, first=first, last=last: e.matmul(
                                            banks[db_][:, :], lhsT=ones_1[:, :], rhs=pb[p_][:, :], start=first, stop=last),
                                            reads=[B_const, B_pb[p_]], writes=[bankB[db_]])
                                d_i = cnt["dn"] % 2
                                cnt["dn"] += 1
                                if window:
                                    T.op("dve", lambda e, d_i=d_i, db_=db_, h=h: e.tensor_tensor(
                                        out=dn[d_i][:, :], in0=banks[db_][:, :], in1=sexp[:, 4 * h:4 * h + 4, :].rearrange("p a b -> p (a b)"), op=ALU.add),
                                        reads=[bankB[db_], B_sexp], writes=[B_dn[d_i]])
                                    T.op("dve", lambda e, d_i=d_i: e.reciprocal(out=dn[d_i][:, :], in_=dn[d_i][:, :]),
                                         reads=[B_dn[d_i]], writes=[B_dn[d_i]])
                                else:
                                    T.op("dve", lambda e, d_i=d_i, db_=db_: e.reciprocal(out=dn[d_i][:, :], in_=banks[db_][:, :]),
                                         reads=[bankB[db_]], writes=[B_dn[d_i]])
                                T.op("dve", lambda e, d_i=d_i, ob_=ob_, h=h, s=s: e.tensor_tensor(
                                    out=osb[:, 4 * h:4 * h + 4, s * 128:(s + 1) * 128], in0=banks[ob_][:, :].rearrange("p (a b) -> p a b", a=4),
                                    in1=dn[d_i][:, :].rearrange("p (a b) -> p a b", a=4), op=ALU.mult),
                                    reads=[bankB[ob_], B_dn[d_i]], writes=B_o[4 * h:4 * h + 4])

                    attn(qa_s[qs], B_qa[qs], oa, B_oa, True)
                    attn(qb_s[qs], B_qb[qs], ob, B_ob, False)

                    for blk in range(4):
                        ws = cnt["w"] % 2
                        cnt["w"] += 1
                        T.dma("sp", "aw%d" % ws, lambda e, ws=ws, blk=blk: e.dma_start(out=wr[ws][:, 0:8, :], in_=Wa[:, :, blk * 512:(blk + 1) * 512]),
                              reads=[B_w[("a", l)]], writes=[B_wr[ws]])
                        T.dma("sp", "aw%d" % ws, lambda e, ws=ws, blk=blk: e.dma_start(out=wr[ws][:, 8:16, :], in_=Wb[:, :, blk * 512:(blk + 1) * 512]),
                              reads=[B_w[("b", l)]], writes=[B_wr[ws]])
                        gs = cnt["g"] % 2
                        cnt["g"] += 1
                        T.dma("sp", "ag%d" % gs, lambda e, gs=gs, blk=blk, c0=c0: e.dma_start(out=gt_s[gs][:, 0:4, :], in_=GT[:, blk * 4:blk * 4 + 4, c0:c0 + T2]),
                              reads=[B_g[t1i]], writes=[B_gt[gs]])
                        T.dma("sp", "ag%d" % gs, lambda e, gs=gs, blk=blk, c0=c0: e.dma_start(out=gt_s[gs][:, 4:8, :], in_=GT[:, 16 + blk * 4:16 + blk * 4 + 4, c0:c0 + T2]),
                              reads=[B_g[t1i]], writes=[B_gt[gs]])
                        for j in range(4):
                            za = cnt["z"] % 6
                            zb = (cnt["z"] + 1) % 6
                            cnt["z"] += 2
                            for k in range(8):
                                T.op("pe", lambda e, ws=ws, j=j, k=k, za=za: e.matmul(
                                    banks[za][:, :T2], lhsT=wr[ws][:, k, j * 128:(j + 1) * 128], rhs=oa[:, k, :], start=(k == 0), stop=(k == 7)),
                                    reads=[B_wr[ws], B_oa[k]], writes=[bankB[za]])
                            for k in range(8):
                                T.op("pe", lambda e, ws=ws, j=j, k=k, zb=zb: e.matmul(
                                    banks[zb][:, :T2], lhsT=wr[ws][:, 8 + k, j * 128:(j + 1) * 128], rhs=ob[:, k, :], start=(k == 0), stop=(k == 7)),
                                    reads=[B_wr[ws], B_ob[k]], writes=[bankB[zb]])
                            t1 = cnt["tf"] % 4
                            t2 = (cnt["tf"] + 1) % 4
                            cnt["tf"] += 2
                            T.op("dve", lambda e, t1=t1, za=za, gs=gs, j=j: e.tensor_tensor(out=tf[t1][:, :], in0=banks[za][:, :T2], in1=gt_s[gs][:, j, :], op=ALU.mult),
                                 reads=[bankB[za], B_gt[gs]], writes=[B_tf[t1]])
                            T.op("dve", lambda e, t2=t2, zb=zb, gs=gs, j=j: e.tensor_tensor(out=tf[t2][:, :], in0=banks[zb][:, :T2], in1=gt_s[gs][:, 4 + j, :], op=ALU.mult),
                                 reads=[bankB[zb], B_gt[gs]], writes=[B_tf[t2]])
                            jj = blk * 4 + j
                            T.op(PENG, lambda e, t1=t1, t2=t2, jj=jj: e.tensor_tensor(out=mg[:, jj, :], in0=tf[t1][:, :], in1=tf[t2][:, :], op=ALU.add),
                                 reads=[B_tf[t1], B_tf[t2]], writes=[B_mg[jj]])
                    for blk in range(4):
                        ws = cnt["w"] % 2
                        cnt["w"] += 1
                        T.dma("sp", "aw%d" % ws, lambda e, ws=ws, blk=blk: e.dma_start(out=wr[ws][:, :, :], in_=Wo[:, :, blk * 512:(blk + 1) * 512]),
                              reads=[B_w[("out", l)]], writes=[B_wr[ws]])
                        for j in range(4):
                            za = cnt["z"] % 6
                            cnt["z"] += 1
                            jj = blk * 4 + j
                            for k in range(KC):
                                T.op("pe", lambda e, ws=ws, j=j, k=k, za=za: e.matmul(
                                    banks[za][:, :T2], lhsT=wr[ws][:, k, j * 128:(j + 1) * 128], rhs=mg[:, k, :], start=(k == 0), stop=(k == KC - 1)),
                                    reads=[B_wr[ws], B_mg[k]], writes=[bankB[za]])
                            q_ = cnt["sq"] % 3
                            cnt["sq"] += 1
                            T.op("act", lambda e, q_=q_, za=za: e.activation(out=sq[q_][:, :], in_=banks[za][:, :T2], func=AF.Square),
                                 reads=[bankB[za]], writes=[B_sq[q_]])
                            T.op("dve", lambda e, za=za, jj=jj, sg=sg: e.tensor_scalar(out=yg[:, jj, :], in0=banks[za][:, :T2], scalar1=GGm[:, l, sg, jj:jj + 1],
                                                                               scalar2=None, op0=ALU.mult),
                                 reads=[bankB[za], B_const], writes=[B_yg[jj]])
                            T.op("pe", lambda e, q_=q_, jj=jj: e.matmul(banks[7][:, :T2], lhsT=ones_d[:, :], rhs=sq[q_][:, :], start=(jj == 0), stop=(jj == KC - 1)),
                                 reads=[B_sq[q_], B_const], writes=[bankB[7]])
                    rsqrt_op(rstd[:, :], banks[7][:, :T2], [bankB[7]], B_rstd)
                    for g4 in range(8):
                        xs_ = cnt["x"] % 2
                        cnt["x"] += 1
                        T.dma("sp", "ax%d" % xs_, lambda e, xs_=xs_, g4=g4, c0=c0: e.dma_start(
                            out=xr[xs_][:, :, :], in_=Xsrc.ap().rearrange("(k p) t -> p k t", p=128)[:, g4 * 2:g4 * 2 + 2, c0:c0 + T2]),
                            reads=B_xs[tt][g4 * 2:g4 * 2 + 2], writes=[B_xr[xs_]])
                        for j in range(2):
                            jj = g4 * 2 + j
                            t1 = cnt["tf"] % 4
                            cnt["tf"] += 1
                            T.op("dve", lambda e, t1=t1, jj=jj: e.tensor_tensor(out=tf[t1][:, :], in0=yg[:, jj, :], in1=rstd[:, :], op=ALU.mult),
                                 reads=[B_yg[jj], B_rstd], writes=[B_tf[t1]])
                            T.op(PENG, lambda e, t1=t1, xs_=xs_, j=j: e.tensor_tensor(out=xo[xs_][:, j, :], in0=tf[t1][:, :], in1=xr[xs_][:, j, :], op=ALU.add),
                                 reads=[B_tf[t1], B_xr[xs_]], writes=[B_xo[xs_]])
                        T.dma("pool", "axo%d" % xs_, lambda e, xs_=xs_, g4=g4, c0=c0: e.dma_start(
                            out=XS.ap().rearrange("(k p) t -> p k t", p=128)[:, g4 * 2:g4 * 2 + 2, c0:c0 + T2], in_=xo[xs_][:, :, :]),
                            reads=[B_xo[xs_]], writes=B_xs[tt][g4 * 2:g4 * 2 + 2])
                T.emit()
            if STOP == 2:
                break

            with contextlib.ExitStack() as ps:
                xy = sb("f_xy", [128, KC, T1], F32, ps)
                uT = sb("f_u", [128, KC, T1], BF16, ps)
                hT = sb("f_h", [128, FC, T1], BF16, ps)
                w13r = [sb("f_w13_%d" % i, [128, KC, 512], BF16, ps) for i in range(2)]
                w2r = [sb("f_w2_%d" % i, [128, FC, 256], BF16, ps) for i in range(2)]
                sq = [sb("f_sq%d" % i, [128, T1], BF16, ps) for i in range(2)]
                tmpf = [sb("f_t%d" % i, [128, T1], F32, ps) for i in range(2)]
                rstd = sb("f_rstd", [128, T1], F32, ps)
                sl = [sb("f_sl%d" % i, [128, T1], BF16, ps) for i in range(2)]
                xr = [sb("f_x%d" % i, [128, T1], F32, ps) for i in range(2)]
                xo = [sb("f_xo%d" % i, [128, T1], F32, ps) for i in range(2)]
                B_xy, B_u, B_h = bufs(KC), bufs(KC), bufs(FC)
                B_w13, B_w2 = bufs(2), bufs(2)
                B_sq, B_tmpf, B_sl = bufs(2), bufs(2), bufs(2)
                B_rstd = Buf()
                B_xr, B_xo = bufs(2), bufs(2)
                cnt = dict(w13=0, w2=0, z=0, sq=0, sl=0, x=0, tf=0)
                W13 = wb_13[l].rearrange("(k p) n -> p k n", p=128)
                W2 = wb_2[l].rearrange("(k p) n -> p k n", p=128)
                for t in range(NT1):
                    c0 = t * T1
                    sg = 0 if t < NST1 else 1
                    xb0, xb1 = B_xs[c0 // T2], B_xs[c0 // T2 + 1]
                    T.dma("sp", "fx", lambda e, c0=c0: e.dma_start(
                        out=xy[:, :, :], in_=XS.ap().rearrange("(k p) t -> p k t", p=128)[:, :, c0:c0 + T1]),
                        reads=xb0 + xb1, writes=B_xy)
                    norm_mod(xy, B_xy, uT, B_u, sq, B_sq, rstd, B_rstd, tmpf, B_tmpf, 7, l, sg, AfT, 48, T1)
                    for blk in range(FC // 2):
                        ws = cnt["w13"] % 2
                        cnt["w13"] += 1
                        T.dma("sp", "fw13_%d" % ws, lambda e, ws=ws, blk=blk: e.dma_start(out=w13r[ws][:, :, 0:256], in_=W13[:, :, blk * 256:(blk + 1) * 256]),
                              reads=[B_w[("13", l)]], writes=[B_w13[ws]])
                        T.dma("sp", "fw13_%d" % ws, lambda e, ws=ws, blk=blk: e.dma_start(out=w13r[ws][:, :, 256:512], in_=W13[:, :, DFF + blk * 256:DFF + (blk + 1) * 256]),
                              reads=[B_w[("13", l)]], writes=[B_w13[ws]])
                        for j in range(2):
                            z1 = cnt["z"] % 6
                            z3 = (cnt["z"] + 1) % 6
                            cnt["z"] += 2
                            for k in range(KC):
                                T.op("pe", lambda e, ws=ws, j=j, k=k, z1=z1: e.matmul(
                                    banks[z1][:, :], lhsT=w13r[ws][:, k, j * 128:(j + 1) * 128], rhs=uT[:, k, :], start=(k == 0), stop=(k == KC - 1)),
                                    reads=[B_w13[ws], B_u[k]], writes=[bankB[z1]])
                            for k in range(KC):
                                T.op("pe", lambda e, ws=ws, j=j, k=k, z3=z3: e.matmul(
                                    banks[z3][:, :], lhsT=w13r[ws][:, k, 256 + j * 128:256 + (j + 1) * 128], rhs=uT[:, k, :], start=(k == 0), stop=(k == KC - 1)),
                                    reads=[B_w13[ws], B_u[k]], writes=[bankB[z3]])
                            s_ = cnt["sl"] % 2
                            cnt["sl"] += 1
                            jj = blk * 2 + j
                            T.op("act", lambda e, s_=s_, z1=z1: e.activation(out=sl[s_][:, :], in_=banks[z1][:, :], func=AF.Silu),
                                 reads=[bankB[z1]], writes=[B_sl[s_]])
                            T.op("dve", lambda e, s_=s_, z3=z3, jj=jj: e.tensor_tensor(out=hT[:, jj, :], in0=banks[z3][:, :], in1=sl[s_][:, :], op=ALU.mult),
                                 reads=[bankB[z3], B_sl[s_]], writes=[B_h[jj]])
                    for blk in range(8):
                        ws = cnt["w2"] % 2
                        cnt["w2"] += 1
                        for k0 in range(0, FC, 11):
                            T.dma("sp", "fw2_%d" % ws, lambda e, ws=ws, blk=blk, k0=k0: e.dma_start(
                                out=w2r[ws][:, k0:k0 + 11, :], in_=W2[:, k0:k0 + 11, blk * 256:(blk + 1) * 256]),
                                reads=[B_w[("2", l)]], writes=[B_w2[ws]])
                        for j in range(2):
                            za = cnt["z"] % 6
                            cnt["z"] += 1
                            jj = blk * 2 + j
                            for k in range(FC):
                                T.op("pe", lambda e, ws=ws, j=j, k=k, za=za: e.matmul(
                                    banks[za][:, :], lhsT=w2r[ws][:, k, j * 128:(j + 1) * 128], rhs=hT[:, k, :], start=(k == 0), stop=(k == FC - 1)),
                                    reads=[B_w2[ws], B_h[k]], writes=[bankB[za]])
                            q_ = cnt["sq"] % 2
                            cnt["sq"] += 1
                            T.op("act", lambda e, q_=q_, za=za: e.activation(out=sq[q_][:, :], in_=banks[za][:, :], func=AF.Square),
                                 reads=[bankB[za]], writes=[B_sq[q_]])
                            T.op("dve", lambda e, za=za, jj=jj, sg=sg: e.tensor_scalar(out=xy[:, jj, :], in0=banks[za][:, :], scalar1=GGf[:, l, sg, jj:jj + 1],
                                                                               scalar2=None, op0=ALU.mult),
                                 reads=[bankB[za], B_const], writes=[B_xy[jj]])
                            T.op("pe", lambda e, q_=q_, jj=jj: e.matmul(banks[7][:, :], lhsT=ones_d[:, :], rhs=sq[q_][:, :], start=(jj == 0), stop=(jj == KC - 1)),
                                 reads=[B_sq[q_], B_const], writes=[bankB[7]])
                    rsqrt_op(rstd[:, :], banks[7][:, :], [bankB[7]], B_rstd)
                    for jj in range(KC):
                        xs_ = cnt["x"] % 2
                        cnt["x"] += 1
                        T.dma("sp", "fxr%d" % xs_, lambda e, xs_=xs_, jj=jj, c0=c0: e.dma_start(
                            out=xr[xs_][:, :], in_=XS[jj * 128:(jj + 1) * 128, c0:c0 + T1]),
                            reads=[xb0[jj], xb1[jj]], writes=[B_xr[xs_]])
                        t1 = cnt["tf"] % 2
                        cnt["tf"] += 1
                        T.op("dve", lambda e, t1=t1, jj=jj: e.tensor_tensor(out=tmpf[t1][:, :], in0=xy[:, jj, :], in1=rstd[:, :], op=ALU.mult),
                             reads=[B_xy[jj], B_rstd], writes=[B_tmpf[t1]])
                        T.op(PENG, lambda e, t1=t1, xs_=xs_: e.tensor_tensor(out=xo[xs_][:, :], in0=tmpf[t1][:, :], in1=xr[xs_][:, :], op=ALU.add),
                             reads=[B_tmpf[t1], B_xr[xs_]], writes=[B_xo[xs_]])
                        T.dma("pool", "fxo%d" % xs_, lambda e, xs_=xs_, jj=jj, c0=c0: e.dma_start(
                            out=Xdst[jj * 128:(jj + 1) * 128, c0:c0 + T1], in_=xo[xs_][:, :]),
                            reads=[B_xo[xs_]], writes=[xb0[jj], xb1[jj]])
                T.emit()

        ends = list(T.chlast.values())
        with nc.Block() as block:
            def fin(eng):
                for d in ends:
                    eng.wait_ge(d.sem, d.count)
            block.sync(fin)
    return nc


def rope_tables(pos):
    pos = np.asarray(pos, np.float32)
    S = pos.shape[0]
    out = np.zeros((4, 128, S), np.float32)
    inv = (np.float32(THETA) ** (-np.arange(0, HD, 2, dtype=np.float32) / np.float32(HD))).astype(np.float32)
    ang = pos[None, :] * inv[:, None]
    c, s = np.cos(ang), np.sin(ang)
    out[0, :64], out[0, 64:] = c, c
    out[1, :64], out[1, 64:] = -s, s
    invh = (np.float32(THETA) ** (-np.arange(0, HD // 2, 2, dtype=np.float32) / np.float32(HD // 2))).astype(np.float32)
    row = np.floor(pos / GRID_W).astype(np.float32)
    col = (pos - row * GRID_W).astype(np.float32)
    for base, p in ((0, row), (64, col)):
        a = p[None, :] * invh[:, None]
        c, s = np.cos(a), np.sin(a)
        out[2, base:base + 32], out[2, base + 32:base + 64] = c, c
        out[3, base:base + 32], out[3, base + 32:base + 64] = -s, s
    return out


def fm(v):
    return np.ascontiguousarray(np.asarray(v, np.float32).reshape(-1, 128).T)


def make_in_maps(inp, cfg=CFG):
    NS, DEPTH = cfg["NS"], cfg["DEPTH"]
    f32 = lambda a: np.ascontiguousarray(np.asarray(a, np.float32))
    xs, xp = f32(inp["x_sample"]), f32(inp["x_prompt"])
    cs, cp = f32(inp["c_sample"]), f32(inp["c_prompt"])
    assert xp.shape[1] == 2 * NS and xs.shape[1] == NS and xs.shape[0] == 8
    gT = np.concatenate([fm(f32(inp[k])[l]) for k in ("g_pre_mix", "g_post_mix", "g_pre_ffn", "g_post_ffn")
                         for l in range(DEPTH)], axis=1)
    bmodT = np.concatenate([fm(f32(inp["b_mod"])[l]) for l in range(DEPTH)], axis=1)
    qkg = np.stack([f32(inp[k])[l] for l in range(DEPTH) for k in ("q_norm_b", "k_norm_b")], axis=1)
    sinkb = np.ascontiguousarray(np.broadcast_to(f32(inp["sink_a"])[:DEPTH].reshape(1, -1), (128, DEPTH * 8)))
    kl = np.arange(128)[:, None]
    ql = np.arange(128)[None, :]
    tri_prev = (kl >= ql).astype(np.float32)
    tri_next = (kl <= ql).astype(np.float32)
    pm = np.zeros((128, 256), np.float32)
    for m_ in range(128):
        pm[(m_ + 64) % 128, m_] = 1.0
        pm[m_ ^ 32, 128 + m_] = 1.0
    shared = dict(gT=f32(gT), bmodT=f32(bmodT), qkg=f32(qkg), sinkb=sinkb, perms=pm)
    wnames = {"w_in": "w_in", "w_branch_a": "w_a", "w_branch_b": "w_b", "w_out": "w_out", "w_13": "w_13", "w_2": "w_2"}
    for k, dk in wnames.items():
        w = f32(inp[k])[:DEPTH]
        shared[dk] = w.reshape(-1, w.shape[-1])
    shared["w_mod"] = f32(inp["w_mod"])[:DEPTH]
    rope_s = rope_tables(np.arange(NS))
    rope_p = rope_tables(np.arange(2 * NS))
    maps = []
    for r in range(NCORES):
        m = dict(shared)
        if r == 0:
            xx, c0_, c1_ = xp[0], cp[0], cp[0]
            rp, valid = rope_p, 1.0
        else:
            i = min(r, 4) - 1
            xx = np.concatenate([xs[2 * i], xs[2 * i + 1]], axis=0)
            c0_, c1_ = cs[2 * i], cs[2 * i + 1]
            rp, valid = np.concatenate([rope_s, rope_s], axis=2), 0.0
        m["xT"] = np.ascontiguousarray(xx.T)
        m["c2T"] = np.ascontiguousarray(np.stack([fm(c0_), fm(c1_)], axis=2).reshape(128, KC * 2))
        m["ropeT"] = np.ascontiguousarray(rp)
        mk = [tri_prev, tri_next, tri_prev * valid, tri_next * valid]
        m["masks"] = np.ascontiguousarray(np.concatenate([np.tile(a, (1, 4)) for a in mk], axis=1))
        bm = np.zeros((128, 4), np.float32)
        if r != 0:
            bm[:, 1] = -30000.0
            bm[:, 2] = -30000.0
        m["biasm"] = bm
        maps.append(m)
    return maps


_NC_CACHE = {}


def kernel(**inputs):
    cfg = CFG
    key = tuple(sorted(cfg.items()))
    if key not in _NC_CACHE:
        _NC_CACHE[key] = build(cfg)
    nc = _NC_CACHE[key]
    in_maps = make_in_maps(inputs, cfg)
    res = run_bass_kernel_spmd(nc, in_maps, core_ids=list(range(NCORES)))
    return assemble(res, cfg)


def assemble(res, cfg=CFG):
    NS = cfg["NS"]
    ys = np.empty((8, NS, D), np.float32)
    yp = np.empty((1, 2 * NS, D), np.float32)
    yp[0] = np.asarray(res.results[0]["yT"], np.float32).T
    for i in range(4):
        y = np.asarray(res.results[1 + i]["yT"], np.float32).T
        ys[2 * i] = y[:NS]
        ys[2 * i + 1] = y[NS:]
    return (yp, ys)
```
